# Optimizing a Trainium2 kernel written in Bass

```python
import jax, jax.numpy as jnp
from jax import lax
import numpy as np

D_MODEL = 1024
BATCH = 8
SEQ = 4096
DEPTH = 4

N_MIXERS = 2
EXPAND = 2
D_INNER = EXPAND * D_MODEL
CONV_WIDTH = 3
HEAD_DIM = 128
N_HEADS = D_INNER // HEAD_DIM
CHUNK = 32
EPS = 1e-6
LB_FLOOR = 1e-30
N_CONV_LAYERS = (DEPTH + N_MIXERS - 1) // N_MIXERS
N_HGRN_LAYERS = (DEPTH + N_MIXERS - 2) // N_MIXERS

kernel_name = "bidir_shortconv_hgrn2_interleaved_trunk"


def rms_norm(x, w):
    xf = x.astype(jnp.float32)
    xf = xf * lax.rsqrt(jnp.mean(xf * xf, axis=-1, keepdims=True) + EPS)
    return (xf * w.astype(jnp.float32)).astype(x.dtype)


def short_conv_mixer(h, w_in, conv_k, w_out):
    proj = h @ w_in
    b_gate, c_gate, u, z = jnp.split(proj, 4, axis=-1)
    v = c_gate * u
    v = lax.conv_general_dilated(
        v, conv_k[:, None, :].astype(v.dtype), window_strides=(1,),
        padding=((CONV_WIDTH // 2, CONV_WIDTH // 2),),
        dimension_numbers=("NWC", "WIO", "NWC"),
        feature_group_count=D_INNER)
    y = b_gate * v * jax.nn.silu(z)
    return y @ w_out


def gla_chunk_scan(q, k, v, log_f):
    bsz, nh, seq, dk = q.shape
    dv = v.shape[-1]
    n_chunks = seq // CHUNK

    def to_chunks(a):
        return jnp.moveaxis(a.reshape(bsz, nh, n_chunks, CHUNK, a.shape[-1]), 2, 0)

    causal_in_chunk = jnp.tril(jnp.ones((CHUNK, CHUNK), dtype=bool))[:, :, None]

    def step(state, inp):
        qc, kc, vc, gc = inp
        g_cum = jnp.cumsum(gc, axis=2)
        o_inter = jnp.einsum("bhtk,bhkv->bhtv", qc * jnp.exp(g_cum), state)
        diff = g_cum[:, :, :, None, :] - g_cum[:, :, None, :, :]
        decay = jnp.where(causal_in_chunk, jnp.exp(jnp.where(causal_in_chunk, diff, 0.0)), 0.0)
        scores = jnp.einsum("bhtsk,bhsk->bhts", qc[:, :, :, None, :] * decay, kc)
        o_intra = jnp.einsum("bhts,bhsv->bhtv", scores, vc)
        g_last = g_cum[:, :, -1:, :]
        new_state = (jnp.exp(g_last[:, :, 0, :])[..., None] * state
                     + jnp.einsum("bhsk,bhsv->bhkv", kc * jnp.exp(g_last - g_cum), vc))
        return new_state, o_inter + o_intra

    s0 = jnp.zeros((bsz, nh, dk, dv), jnp.float32)
    _, o = lax.scan(step, s0, (to_chunks(q), to_chunks(k), to_chunks(v), to_chunks(log_f)))
    return jnp.moveaxis(o, 0, 2).reshape(bsz, nh, seq, dv)


def hgrn_lower_bounds(lb_logits):
    p = jax.nn.softmax(lb_logits.astype(jnp.float32), axis=0)
    return jnp.cumsum(p, axis=0) - p[0]


def hgrn2_mixer(h, w_in, lb, norm_w, w_out):
    bsz, seq, _ = h.shape
    proj = h @ w_in
    q, f_fw, f_bw, i_val, z = jnp.split(proj, 5, axis=-1)
    lb = jnp.clip(lb, 0.0, 1.0 - 1e-6)
    log_lb = jnp.log(jnp.maximum(lb, LB_FLOOR))
    log_1m_lb = jnp.log1p(-lb)

    def heads(a):
        return a.astype(jnp.float32).reshape(bsz, seq, N_HEADS, HEAD_DIM).transpose(0, 2, 1, 3)

    def gates(f_pre):
        fp = f_pre.astype(jnp.float32)
        log_f = jnp.logaddexp(log_lb, log_1m_lb + jax.nn.log_sigmoid(fp))
        key = (1.0 - lb) * jax.nn.sigmoid(-fp)
        return heads(key), heads(log_f)

    qh = heads(q) * (HEAD_DIM ** -0.5)
    vh = heads(i_val)
    k_fw, lf_fw = gates(f_fw)
    k_bw, lf_bw = gates(f_bw)
    o_fw = gla_chunk_scan(qh, k_fw, vh, lf_fw)
    flip = lambda a: jnp.flip(a, axis=2)
    o_bw = flip(gla_chunk_scan(flip(qh), flip(k_bw), flip(vh), flip(lf_bw)))
    o = o_fw + o_bw
    o = o * lax.rsqrt(jnp.mean(o * o, axis=-1, keepdims=True) + EPS)
    o = o.transpose(0, 2, 1, 3).reshape(bsz, seq, D_INNER) * norm_w.astype(jnp.float32)
    y = o.astype(h.dtype) * jax.nn.silu(z)
    return y @ w_out


def setup_inputs(seed: int = 0) -> dict:
    key = jax.random.key(seed)
    ks = jax.random.split(key, 10)
    nrm = jax.random.normal
    f32 = jnp.float32
    x = nrm(ks[0], (BATCH, SEQ, D_MODEL), f32)
    norm_w = 1.0 + 0.02 * nrm(ks[1], (DEPTH, D_MODEL), f32)
    final_norm_w = 1.0 + 0.02 * nrm(ks[2], (D_MODEL,), f32)
    conv_w_in = nrm(ks[3], (N_CONV_LAYERS, D_MODEL, 4 * D_INNER), f32) * D_MODEL ** -0.5
    conv_kernel = nrm(ks[4], (N_CONV_LAYERS, CONV_WIDTH, D_INNER), f32) * CONV_WIDTH ** -0.5
    conv_w_out = nrm(ks[5], (N_CONV_LAYERS, D_INNER, D_MODEL), f32) * D_INNER ** -0.5
    hgrn_w_in = nrm(ks[6], (N_HGRN_LAYERS, D_MODEL, 5 * D_INNER), f32) * D_MODEL ** -0.5
    hgrn_lb_logits = 0.1 * nrm(ks[7], (N_HGRN_LAYERS, D_INNER), f32)
    hgrn_norm_w = 1.0 + 0.02 * nrm(ks[8], (N_HGRN_LAYERS, D_INNER), f32)
    hgrn_w_out = nrm(ks[9], (N_HGRN_LAYERS, D_INNER, D_MODEL), f32) * D_INNER ** -0.5
    return {"x": x, "norm_w": norm_w, "final_norm_w": final_norm_w,
            "conv_w_in": conv_w_in, "conv_kernel": conv_kernel, "conv_w_out": conv_w_out,
            "hgrn_w_in": hgrn_w_in, "hgrn_lb_logits": hgrn_lb_logits,
            "hgrn_norm_w": hgrn_norm_w, "hgrn_w_out": hgrn_w_out}


def reference(x, norm_w, final_norm_w, conv_w_in, conv_kernel, conv_w_out,
              hgrn_w_in, hgrn_lb_logits, hgrn_norm_w, hgrn_w_out):
    lower_bounds = hgrn_lower_bounds(hgrn_lb_logits)
    for layer in range(DEPTH):
        h = rms_norm(x, norm_w[layer])
        j = layer // N_MIXERS
        if layer % N_MIXERS == 0:
            y = short_conv_mixer(h, conv_w_in[j], conv_kernel[j], conv_w_out[j])
        else:
            y = hgrn2_mixer(h, hgrn_w_in[j], lower_bounds[j], hgrn_norm_w[j], hgrn_w_out[j])
        x = x + y
    return rms_norm(x, final_norm_w)
```

```python
import contextlib
import numpy as np
import concourse.bass as bass
import concourse.mybir as mybir
from concourse.bass_utils import run_bass_kernel_spmd
from concourse.ap import AP

F32 = mybir.dt.float32
BF16 = mybir.dt.bfloat16
AF = mybir.ActivationFunctionType
ALU = mybir.AluOpType

T = 4096
D = 1024
E = 2048
NJ = 16
NT = 8
TS = 512
NSUB = 32
CH = 64
EPS = 1e-6
DEPTH = 4
EPOCH = 30000
DMA_ROT = 8
HG_HEADS = 16
HG_PHASES = 3


class Op:
    __slots__ = ("eng", "fn", "dma", "waits", "signal", "sem", "val", "eidx", "deps", "gidx")

    def __init__(self, eng, fn, dma):
        self.eng = eng
        self.fn = fn
        self.dma = dma
        self.waits = []
        self.signal = dma
        self.sem = None
        self.val = None
        self.deps = []


class Prog:
    ENGS = ("pe", "act", "dve", "pool", "sp")

    def __init__(self):
        self.ops = []
        self.eng_ops = {e: [] for e in self.ENGS}
        self.last_w = {}
        self.readers = {}

    def add(self, eng, fn, reads=(), writes=(), dma=False):
        op = Op(eng, fn, dma)
        op.gidx = len(self.ops)
        op.eidx = len(self.eng_ops[eng])
        deps = {}

        def need(p, typ):
            if p is op:
                return
            if p.dma or op.dma or p.eng != op.eng or typ == "RAW":
                deps[id(p)] = p

        for k in reads:
            p = self.last_w.get(k)
            if p is not None:
                need(p, "RAW")
        for k in writes:
            p = self.last_w.get(k)
            if p is not None:
                need(p, "WAW")
            for r in self.readers.get(k, {}).values():
                if isinstance(r, list):
                    for rr in r:
                        need(rr, "WAR")
                else:
                    need(r, "WAR")
        for k in writes:
            self.last_w[k] = op
            self.readers[k] = {}
        for k in reads:
            d = self.readers.setdefault(k, {})
            if dma:
                d.setdefault("dma", []).append(op)
            else:
                d[eng] = op
        op.deps = list(deps.values())
        self.ops.append(op)
        self.eng_ops[eng].append(op)
        return op

    def finalize(self, nc, stack):
        seen = {e: {f: -1 for f in self.ENGS} for e in self.ENGS}
        seen_dma = {e: set() for e in self.ENGS}
        for op in self.ops:
            e = op.eng
            for p in sorted(op.deps, key=lambda q: q.gidx):
                if p.dma:
                    if id(p) in seen_dma[e]:
                        continue
                    seen_dma[e].add(id(p))
                    op.waits.append(p)
                else:
                    if seen[e][p.eng] >= p.eidx:
                        continue
                    seen[e][p.eng] = p.eidx
                    p.signal = True
                    op.waits.append(p)
        for e in self.ENGS:
            cnt = 0
            sems = []
            dcnt = 0
            dsems = []
            dlast = {}
            for op in self.eng_ops[e]:
                if op.dma:
                    slot = dcnt % DMA_ROT
                    if slot >= len(dsems):
                        dsems.append(stack.enter_context(nc.semaphore("d_%s_%d" % (e, slot))))
                    prev = dlast.get(slot)
                    if prev is not None and all(w is not prev for w in op.waits):
                        op.waits.append(prev)
                    op.sem = dsems[slot]
                    op.val = 16 * (dcnt // DMA_ROT + 1)
                    dlast[slot] = op
                    dcnt += 1
                elif op.signal:
                    ep = cnt // EPOCH
                    if ep >= len(sems):
                        sems.append(stack.enter_context(nc.semaphore("c_%s_%d" % (e, ep))))
                    op.sem = sems[ep]
                    op.val = cnt % EPOCH + 1
                    cnt += 1

    def emit(self, eng, h):
        for op in self.eng_ops[eng]:
            for p in op.waits:
                h.wait_ge(p.sem, p.val)
            ins = op.fn(h)
            if op.dma:
                ins.then_inc(op.sem, 16)
            elif op.signal:
                ins.then_inc(op.sem, 1)


def cap(base, offset, dims):
    a = base.ap
    return AP(base.tensor, base.offset + offset, [list(a[0])] + [list(d) for d in dims])


def rev2d(ap):
    a = ap.ap
    assert len(a) == 2
    s, n = a[1]
    return AP(ap.tensor, ap.offset + s * (n - 1), [list(a[0]), [-s, n]])


def build_program(layers, first, last):
    nc = bass.Bass("TRN2", target_bir_lowering=False)
    P = Prog()
    st = contextlib.ExitStack()

    def din(name, shape, dt=F32):
        return nc.dram_tensor(name, list(shape), dt, kind="ExternalInput").ap()

    x_in = din("x", [T, D])
    norm_w = din("norm_w", [DEPTH, D])
    final_norm_w = din("final_norm_w", [1, D])
    conv_w_in = din("conv_w_in", [2, D, 4 * E])
    conv_kernel = din("conv_kernel", [2, 3, E])
    conv_w_out = din("conv_w_out", [2, E, D])
    hgrn_w_in = din("hgrn_w_in", [2, D, 5 * E])
    hgrn_lb_logits = din("hgrn_lb_logits", [2, E])
    hgrn_norm_w = din("hgrn_norm_w", [2, E])
    hgrn_w_out = din("hgrn_w_out", [2, E, D])
    out = nc.dram_tensor("out", [T, D], F32, kind="ExternalOutput").ap()
    xs = nc.dram_tensor("xs_scr", [T, D], F32, kind="Internal").ap()
    ys = nc.dram_tensor("ys_scr", [NJ, 128, T], BF16, kind="Internal").ap()

    def sb(name, shape, dt=F32):
        return st.enter_context(nc.sbuf_tensor(name, list(shape), dt))

    hT = sb("hT", [128, 8, T], BF16)
    big4 = sb("big4", [128, 4, T], BF16)
    wbuf = [sb("wbuf%d" % i, [128, 5, 8, 128], BF16) for i in range(2)]
    ident = sb("ident", [128, 128], BF16)
    identf = sb("identf", [128, 128], F32)
    onesf = sb("onesf", [128, 128], F32)
    par_a = sb("par_a", [96, 128], F32)
    par_b = sb("par_b", [64, 128], F32)
    ck = sb("ck", [128, 96], F32)
    pb = sb("pb", [128, 64], F32)
    lbt = sb("lbt", [128, 2, 16], F32)
    omlt = sb("omlt", [128, 2, 16], F32)
    nomlt = sb("nomlt", [128, 2, 16], F32)
    mask2 = sb("mask2", [128, 2, 4, CH], F32)
    m01 = sb("m01", [128, TS + CH], F32)
    ystg = [sb("ystg%d" % i, [128, TS], BF16) for i in range(2)]
    ssq = sb("ssq", [128, 8], F32)
    dummy = sb("dummy_t", [128, 8], F32)
    scrA = sb("scrA", [128, T + 4], F32)
    scrB = sb("scrB", [128, T], F32)
    scrC = sb("scrC", [128, T], F32)
    vbuf = scrA[:, 0:T + 2]
    kdT = scrA[:, 0:T].bitcast(BF16).rearrange("p (d s k) -> p d s k", d=2, s=NSUB)
    gbuf = scrB[:, 0:T]
    sbt = scrB[:, 0:T].bitcast(BF16).rearrange("p (c v) -> p c v", v=128)
    ytile = [scrA[:, 0:T].bitcast(BF16).rearrange("p (j t) -> p j t", j=NJ),
             scrB[:, 0:T].bitcast(BF16).rearrange("p (j t) -> p j t", j=NJ)]
    tmpf = [scrC[:, k * TS:(k + 1) * TS] for k in range(8)]
    xt = [scrC[:, 0:1024], scrC[:, 1024:2048]]
    hb = [scrC[:, 2048:2560].bitcast(BF16), scrC[:, 2560:3072].bitcast(BF16)]
    wbc = scrC[:, 3072:4096]
    vsb = sb("vsb", [128, NSUB, 128], BF16)
    sft = [sb("sft%d" % i, [128, 8, 128], BF16) for i in range(2)]
    stF = sb("stF", [128, 128 * 9], F32)
    stB = sb("stB", [128, 128 * 9], F32)
    abc = sb("abc", [128, 128 * 9], F32)
    kdfm = [sb("kdfm%d" % i, [128, 2, TS], BF16) for i in range(2)]
    scsb = [sb("scsb%d" % i, [128, 2, 4, CH], BF16) for i in range(2)]
    eas = sb("eas", [128, 2, NT, 9], F32)
    eRs = sb("eRs", [128, 2, NT, 8], F32)

    ps = [st.enter_context(nc.psum_tensor("ps%d" % i, [128, 512], F32)) for i in range(8)]
    ps_ctr = [0]

    def bank():
        b = ps_ctr[0] % 8
        ps_ctr[0] += 1
        return b

    REGIONS = ("scrA", "scrB", "scrC")

    def rtag(reads, *aps):
        reads = list(reads)
        for a in aps:
            if a is None or not hasattr(a, "tensor"):
                continue
            n = a.tensor.name
            if n in REGIONS and ("REG", n) not in reads:
                reads.append(("REG", n))
        return reads

    def act(out_, in_, func, reads, writes, scale=None, bias=None, accum=None):
        kw = {}
        if scale is not None:
            kw["scale"] = scale
        if bias is not None:
            kw["bias"] = bias
        if accum is not None:
            kw["accum_out"] = accum
        P.add("act", lambda h: h.activation(out=out_, in_=in_, func=func, **kw),
              rtag(reads, out_, in_, accum), writes)

    def mm(out_, lhsT, rhs, start, stop, reads, writes):
        P.add("pe", lambda h: h.matmul(out_, lhsT=lhsT, rhs=rhs, start=start, stop=stop),
              rtag(reads, lhsT, rhs), writes)

    def tr(out_, in_, idn, reads, writes):
        P.add("pe", lambda h: h.transpose(out_, in_, idn), rtag(reads, in_), writes)

    def dma(eng, out_, in_, reads, writes):
        P.add(eng, lambda h: h.dma_start(out=out_, in_=in_), rtag(reads, out_, in_), writes, dma=True)

    def tt(eng, out_, in0, in1, op, reads, writes):
        P.add(eng, lambda h: h.tensor_tensor(out=out_, in0=in0, in1=in1, op=op),
              rtag(reads, out_, in0, in1), writes)

    def tsc(eng, out_, in0, s1, s2, op0, op1, reads, writes):
        if op1 is None:
            P.add(eng, lambda h: h.tensor_scalar(out=out_, in0=in0, scalar1=s1, scalar2=None, op0=op0),
                  rtag(reads, out_, in0), writes)
        else:
            P.add(eng, lambda h: h.tensor_scalar(out=out_, in0=in0, scalar1=s1, scalar2=s2, op0=op0,
                                                 op1=op1), rtag(reads, out_, in0), writes)

    def stt(out_, in0, scalar, in1, op0, op1, reads, writes):
        P.add("dve", lambda h: h.scalar_tensor_tensor(out=out_, in0=in0, scalar=scalar, in1=in1,
                                                      op0=op0, op1=op1),
              rtag(reads, out_, in0, in1), writes)

    def scan(out_, d0, d1, reads, writes):
        P.add("dve", lambda h: h.tensor_tensor_scan(out=out_, data0=d0, data1=d1, initial=0.0,
                                                    op0=ALU.mult, op1=ALU.add),
              rtag(reads, out_, d0, d1), writes)

    def red(out_, in_, reads, writes):
        P.add("dve", lambda h: h.tensor_reduce(out=out_, in_=in_, axis=mybir.AxisListType.X, op=ALU.add),
              rtag(reads, out_, in_), writes)

    def cpy(eng, out_, in_, reads, writes):
        P.add(eng, lambda h: h.tensor_copy(out=out_, in_=in_), rtag(reads, out_, in_), writes)

    def mset(eng, ap, val, reads, writes):
        P.add(eng, lambda h: h.memset(ap, val), rtag(reads, ap), writes)

    def region_barrier():
        for n in REGIONS:
            P.add("pool", lambda h: h.memset(dummy[:, 0:1], 0.0), [], [("REG", n)])

    mset("pool", identf[:], 0.0, [], ["identf"])
    P.add("pool", lambda h: h.affine_select(out=identf[:], in_=identf[:], pattern=[[-1, 128]],
                                            compare_op=ALU.not_equal, fill=1.0, base=0,
                                            channel_multiplier=1), ["identf"], ["identf"])
    cpy("pool", ident[:], identf[:], ["identf"], ["ident"])
    mset("pool", onesf[:], 1.0, [], ["onesf"])
    mset("pool", m01[:], 1.0, [], ["m01"])
    mset("pool", m01[:, 0:TS + 1:CH], 0.0, ["m01"], ["m01"])
    mset("pool", eas[:], 0.0, [], [("eas", d, i) for d in range(2) for i in range(NT)])
    mset("pool", mask2[:], 1.0, [], ["mask2"])
    P.add("pool", lambda h: h.affine_select(out=mask2[0:64, 0, :, :], in_=mask2[0:64, 0, :, :],
                                            pattern=[[0, 4], [1, CH]], compare_op=ALU.is_ge, fill=0.0,
                                            base=0, channel_multiplier=-1), ["mask2"], ["mask2"])
    P.add("pool", lambda h: h.affine_select(out=mask2[0:64, 1, :, :], in_=mask2[0:64, 1, :, :],
                                            pattern=[[0, 4], [-1, CH]], compare_op=ALU.is_ge, fill=0.0,
                                            base=0, channel_multiplier=1), ["mask2"], ["mask2"])
    dma("sp", mask2[64:128, :, :, :], mask2[0:64, :, :, :], ["mask2"], ["mask2"])

    dma("sp", par_a[:], conv_kernel.rearrange("l k (j p) -> (l k j) p", p=128), [], ["par_a"])
    dma("sp", par_b[0:32, :], hgrn_lb_logits.rearrange("l (j p) -> (l j) p", p=128), [], ["par_b0"])
    dma("sp", par_b[32:64, :], hgrn_norm_w.rearrange("l (j p) -> (l j) p", p=128), [], ["par_b1"])
    b = bank()
    tr(ps[b][:, 0:96], par_a[:], identf[0:96, 0:96], ["par_a", "identf"], [("ps", b)])
    cpy("dve", ck[:], ps[b][:, 0:96], [("ps", b)], ["ck"])
    b = bank()
    tr(ps[b][:, 0:64], par_b[:], identf[0:64, 0:64], ["par_b0", "par_b1", "identf"], [("ps", b)])
    cpy("dve", pb[:], ps[b][:, 0:64], [("ps", b)], ["pb"])
    mset("dve", lbt[:], 0.0, [], ["lbt"])
    tt("dve", lbt[:, 1, :], pb[:, 16:32], pb[:, 0:16], ALU.subtract, ["pb", "lbt"], ["lbt"])
    act(lbt[:, 1, :], lbt[:, 1, :], AF.Sigmoid, ["lbt"], ["lbt"])
    tsc("dve", lbt[:], lbt[:], 0.0, 1.0 - 1e-6, ALU.max, ALU.min, ["lbt"], ["lbt"])
    tsc("dve", omlt[:], lbt[:], -1.0, 1.0, ALU.mult, ALU.add, ["lbt"], ["omlt"])
    tsc("dve", nomlt[:], lbt[:], 1.0, -1.0, ALU.mult, ALU.add, ["lbt"], ["nomlt"])

    def norm_tail(s, final):
        par = s % 2
        xn = xt[par]
        xk = ("xt", par)
        col = s % 8
        sq = ssq[:, col:col + 1]
        act(hb[par], xn, AF.Square, [xk], [("hb", par), ("ssq", col)], accum=sq)
        act(sq, sq, AF.Ln, [("ssq", col)], [("ssq", col)], scale=1.0 / D, bias=EPS)
        act(sq, sq, AF.Exp, [("ssq", col)], [("ssq", col)], scale=-0.5)
        if final:
            stt(xn, xn, sq, wbc, ALU.mult, ALU.mult, [xk, ("ssq", col), "wbc"], [xk])
            dma("sp", out[s * 128:(s + 1) * 128, :], xn, [xk], [("outd", s)])
            return
        stt(hb[par], xn, sq, wbc, ALU.mult, ALU.mult, [xk, ("ssq", col), "wbc"], [("hb", par)])
        b = bank()
        pst = ps[b][:].bitcast(BF16)
        for c in range(8):
            tr(pst[:, c * 128:(c + 1) * 128], hb[par][:, c * 128:(c + 1) * 128], ident[:],
               [("hb", par), "ident"], [("ps", b)])
        act(hT[:, :, s * 128:(s + 1) * 128], pst.rearrange("p (c t) -> p c t", c=8), AF.Copy,
            [("ps", b)], [("hT", s)])

    def load_wbc(src_row):
        dma("sp", wbc, src_row.partition_broadcast(128), [], ["wbc"])

    def phase0(layer):
        load_wbc(norm_w[layer:layer + 1, :])
        for s in range(NSUB):
            dma("sp", xt[s % 2], x_in[s * 128:(s + 1) * 128, :], [], [("xt", s % 2)])
            norm_tail(s, False)

    def load_wout(w_out_l):
        for m in range(4):
            dma("pool", big4[:, m, :].rearrange("p (j d) -> p j d", j=4),
                w_out_l[m * 512:(m + 1) * 512, :].rearrange("(j p) d -> p j d", p=128),
                [], [("b4", m)])

    def phase2(xsrc, next_w_row, final):
        load_wbc(next_w_row)
        wout = big4[:].rearrange("p m (j d) -> p (m j) d", j=4)
        for i in range(NT):
            ybt = ytile[i % 2]
            ykey = ("ytile", i % 2)
            dma("sp", ybt, ys[:, :, i * TS:(i + 1) * TS].rearrange("j p t -> p j t"),
                [("ysd", j, i) for j in range(NJ)], [ykey])
            for ss in range(4):
                s = i * 4 + ss
                par = s % 2
                x_t = xt[par]
                dma("sp", x_t, xsrc[s * 128:(s + 1) * 128, :], [("xsd", s)], [("xt", par)])
                bks = []
                for dh in range(2):
                    b = bank()
                    bks.append(b)
                    for j in range(NJ):
                        mm(ps[b][:, :], ybt[:, j, ss * 128:(ss + 1) * 128],
                           wout[:, j, dh * 512:(dh + 1) * 512], j == 0, j == NJ - 1,
                           [ykey, ("b4", j // 4)], [("ps", b)])
                for dh in range(2):
                    b = bks[dh]
                    tt("dve", x_t[:, dh * 512:(dh + 1) * 512], ps[b][:, :], x_t[:, dh * 512:(dh + 1) * 512],
                       ALU.add, [("ps", b), ("xt", par)], [("xt", par)])
                if not final:
                    dma("sp", xs[s * 128:(s + 1) * 128, :], x_t, [("xt", par)], [("xsd", s)])
                norm_tail(s, final)

    wslot = [0]

    def load_w(w_in_l, j, ngroups):
        slot = wslot[0] % 2
        wslot[0] += 1
        for g in range(ngroups):
            dma("pool", wbuf[slot][:, g, :, :],
                w_in_l[:, g * E + j * 128: g * E + (j + 1) * 128].rearrange("(c p) e -> p c e", p=128),
                [], [("w", slot, g)])
        return slot

    def proj_fm(slot, g, i):
        b = bank()
        for c in range(8):
            mm(ps[b][:, :], wbuf[slot][:, g, c, :], hT[:, c, i * TS:(i + 1) * TS], c == 0, c == 7,
               [("w", slot, g)] + [("hT", i * 4 + q) for q in range(4)], [("ps", b)])
        return b

    def conv_tile(cl, j, i):
        cv = tmpf[4 + i % 2]
        ckey = ("tmpf", 4 + i % 2)
        k0 = ck[:, cl * 48 + 0 * 16 + j: cl * 48 + 0 * 16 + j + 1]
        k1 = ck[:, cl * 48 + 1 * 16 + j: cl * 48 + 1 * 16 + j + 1]
        k2 = ck[:, cl * 48 + 2 * 16 + j: cl * 48 + 2 * 16 + j + 1]
        lo = 1 + i * TS
        vr = [("v", q) for q in (i - 1, i, i + 1) if 0 <= q < NT] + ["vpad", "ck"]
        tsc("dve", cv, vbuf[:, lo:lo + TS], k1, None, ALU.mult, None, vr, [ckey])
        stt(cv, vbuf[:, lo - 1:lo - 1 + TS], k0, cv, ALU.mult, ALU.add, vr + [ckey], [ckey])
        stt(cv, vbuf[:, lo + 1:lo + 1 + TS], k2, cv, ALU.mult, ALU.add, vr + [ckey], [ckey])
        yst = ystg[i % 2]
        tt("dve", yst[:], cv, gbuf[:, i * TS:(i + 1) * TS], ALU.mult, [ckey, ("g", i)], [("ystg", i % 2)])
        dma("sp", ys[j][:, i * TS:(i + 1) * TS], yst[:], [("ystg", i % 2)], [("ysd", j, i)])

    def conv_phase1(cl):
        w_in_l = conv_w_in[cl]
        mset("pool", vbuf[:, 0:1], 0.0, [], ["vpad"])
        mset("pool", vbuf[:, T + 1:T + 2], 0.0, ["vpad"], ["vpad"])
        slot_next = load_w(w_in_l, 0, 4)
        for j in range(NJ):
            slot = slot_next
            if j + 1 < NJ:
                slot_next = load_w(w_in_l, j + 1, 4)
            for i in range(NT):
                bb = proj_fm(slot, 0, i)
                bc = proj_fm(slot, 1, i)
                bu = proj_fm(slot, 2, i)
                bz = proj_fm(slot, 3, i)
                szt = tmpf[i % 2]
                ct = tmpf[2 + i % 2]
                act(szt, ps[bz][:, :], AF.Silu, [("ps", bz)], [("tmpf", i % 2)])
                act(ct, ps[bc][:, :], AF.Copy, [("ps", bc)], [("tmpf", 2 + i % 2)])
                lo = 1 + i * TS
                tt("dve", vbuf[:, lo:lo + TS], ps[bu][:, :], ct, ALU.mult,
                   [("ps", bu), ("tmpf", 2 + i % 2)], [("v", i)])
                tt("dve", gbuf[:, i * TS:(i + 1) * TS], ps[bb][:, :], szt, ALU.mult,
                   [("ps", bb), ("tmpf", i % 2)], [("g", i)])
                if i >= 1:
                    conv_tile(cl, j, i - 1)
            conv_tile(cl, j, NT - 1)

    QK = [(big4[:, 0, :], big4[:, 1, :]), (big4[:, 2, :], big4[:, 3, :])]
    ORDER = list(range(NT - 1, -1, -1))

    def kd_transposes(i):
        b = bank()
        pst = ps[b][:].bitcast(BF16)
        kd = kdfm[i % 2]
        for d in range(2):
            for ss in range(4):
                col = (d * 4 + ss) * 128
                tr(pst[:, col:col + 128], kd[:, d, ss * 128:(ss + 1) * 128], ident[:],
                   [("kdfm", i % 2, d), "ident"], [("ps", b)])
        act(kdT[:, :, i * 4:(i + 1) * 4, :], pst.rearrange("p (d s k) -> p d s k", d=2, s=4),
            AF.Copy, [("ps", b)], [("kdT", i)])

    dAR = sb("dAR", [128, 2, 8], F32)
    hs = sb("hs", [128, 2, 8], F32)
    ecs = sb("ecs", [128, 2, 8], F32)
    XB, YB = stF, stB

    def delta_s(i, d):
        for half in range(2):
            b = bank()
            for cc in range(4):
                c = cc * 2 + half
                s = i * 4 + cc
                mm(ps[b][:, cc * 128:(cc + 1) * 128],
                   kdT[half * 64:(half + 1) * 64, d, s, :], vsb[half * 64:(half + 1) * 64, s, :],
                   True, True, [("kdT", i), ("vsb", i)], [("ps", b)])
            off = (1 if d == 0 else 0) + half
            dst = cap(XB[:], off, [[2, 4], [9, 128]])
            src = ps[b][:, :].rearrange("p (c v) -> p c v", c=4)
            if half == 0:
                act(dst, src, AF.Copy, [("ps", b)], [("XB", "s", half)])
            else:
                cpy("dve", dst, src, [("ps", b)], [("XB", "s", half)])

    XKEYS = [("XB", "s", 0), ("XB", "s", 1), ("XB", "c")]

    def hgrn_head(hl, j, slot):
        hnw_j = pb[:, 32 + hl * 16 + j: 32 + hl * 16 + j + 1]
        lb_j = lbt[:, hl, j:j + 1]
        oml_j = omlt[:, hl, j:j + 1]
        noml_j = nomlt[:, hl, j:j + 1]
        gs = -1.0 if hl == 0 else 1.0
        qf_, kf_ = QK[0]
        qb_, kb_ = QK[1]

        def proj_to(bnk, g, i):
            for c in range(8):
                mm(ps[bnk][:, :], wbuf[slot][:, g, c, :], hT[:, c, i * TS:(i + 1) * TS], c == 0, c == 7,
                   [("w", slot, g)] + [("hT", i * 4 + q) for q in range(4)], [("ps", bnk)])

        def kd_tr(i, bnk):
            pst = ps[bnk][:].bitcast(BF16)
            kd = kdfm[i % 2]
            for d in range(2):
                for ss in range(4):
                    col = (d * 4 + ss) * 128
                    tr(pst[:, col:col + 128], kd[:, d, ss * 128:(ss + 1) * 128], ident[:],
                       [("kdfm", i % 2, d), "ident"], [("ps", bnk)])
            act(kdT[:, :, i * 4:(i + 1) * 4, :], pst.rearrange("p (d s k) -> p d s k", d=2, s=4),
                AF.Copy, [("ps", bnk)], [("kdT", i)])

        for n, i in enumerate(ORDER):
            p = n % 2
            bff, bfb, bq, bv = 4 * p, 4 * p + 1, 4 * p + 2, 4 * p + 3
            bf = [bff, bfb]
            proj_to(bff, 1, i)
            proj_to(bfb, 2, i)
            proj_to(bq, 0, i)
            for ss in range(4):
                for c in range(8):
                    mm(ps[bv][:, ss * 128:(ss + 1) * 128],
                       hT[:, c, i * TS + ss * 128: i * TS + (ss + 1) * 128], wbuf[slot][:, 3, c, :],
                       c == 0, c == 7, [("w", slot, 3), ("hT", i * 4 + ss)], [("ps", bv)])
            if n >= 1:
                kd_tr(ORDER[n - 1], 4 * (1 - p) + 3)
            cpy("dve", vsb[:, i * 4:(i + 1) * 4, :], ps[bv][:, :].rearrange("p (s v) -> p s v", s=4),
                [("ps", bv)], [("vsb", i)])
            S = [tmpf[0], tmpf[4]]
            L = [tmpf[1], tmpf[5]]
            Kb = [tmpf[2], tmpf[6]]
            G = [tmpf[3], tmpf[7]]
            kS = [("tmpf", 0), ("tmpf", 4)]
            kL = [("tmpf", 1), ("tmpf", 5)]
            kK = [("tmpf", 2), ("tmpf", 6)]
            kG = [("tmpf", 3), ("tmpf", 7)]
            rcol = [CH // 2 - 1, CH // 2]
            acol = [CH - 1, 0]
            for d in range(2):
                act(S[d], ps[bf[d]][:, :], AF.Exp, [("ps", bf[d])], [kS[d]], scale=-1.0)
            for d in range(2):
                act(L[d], S[d], AF.Ln, [kS[d]], [kL[d]], bias=1.0)
            for d in range(2):
                act(S[d], L[d], AF.Exp, [kL[d]], [kS[d]], scale=-1.0)
            if hl != 0:
                for d in range(2):
                    act(L[d], S[d], AF.Ln, [kS[d], "lbt", "omlt"], [kL[d]], scale=oml_j, bias=lb_j)
            for d in range(2):
                tsc("dve", Kb[d], S[d], noml_j, oml_j, ALU.mult, ALU.add, [kS[d], "omlt", "nomlt"], [kK[d]])
            for d in range(2):
                L3 = L[d].rearrange("p (c t) -> p c t", t=CH)
                half_view = L3[:, :, 0:CH // 2] if d == 0 else L3[:, :, CH // 2:CH]
                red(hs[:, d, :], half_view, [kL[d]], [("hs", d)])
            for d in range(2):
                Lpos = cap(L[d], 0 if d == 0 else CH - 1, [[CH, 8]])
                tt("dve", Lpos, Lpos, hs[:, d, :], ALU.subtract, [kL[d], ("hs", d)], [kL[d]])
            scan(G[0], m01[:, 0:TS], L[0], [kL[0], "m01"], [kG[0]])
            scan(rev2d(G[1]), rev2d(m01[:, 1:TS + 1]), rev2d(L[1]), [kL[1], "m01"], [kG[1]])
            for d in range(2):
                Av = cap(G[d], acol[d], [[CH, 8]])
                tt("dve", dAR[:, d, :], Av, hs[:, d, :], ALU.add, [kG[d], ("hs", d)], [("dAR", d)])
                ea_dst = eas[:, 0, i, 1:9] if d == 0 else eas[:, 1, i, 0:8]
                act(ecs[:, d, :], Av, AF.Exp, [kG[d]], [("ecs", d)], scale=gs)
                act(eRs[:, d, i, :], hs[:, d, :], AF.Exp, [("hs", d)], [("eRs", d, i)], scale=gs)
                act(ea_dst, dAR[:, d, :], AF.Exp, [("dAR", d)], [("eas", d, i)], scale=gs)
            for d in range(2):
                act(L[d], G[d], AF.Exp, [kG[d]], [kL[d]], scale=gs)
            for d in range(2):
                qd, _ = QK[d]
                stt(qd[:, i * TS:(i + 1) * TS], ps[bq][:, :], 128.0 ** -0.5, L[d], ALU.mult, ALU.mult,
                    [("ps", bq), kL[d]], [("b4", 2 * d)])
            for d in range(2):
                act(S[d], G[d], AF.Exp, [kG[d], kK[d]], [kS[d]], scale=-gs)
            for d in range(2):
                G3 = G[d].rearrange("p (c t) -> p c t", t=CH)
                K3 = Kb[d].rearrange("p (c t) -> p c t", t=CH)
                tt("pool", G3, K3, cap(ecs[:, d, :], 0, [[1, 8], [0, CH]]), ALU.mult,
                   [kK[d], kS[d], kL[d], ("ecs", d)], [kG[d]])
            for d in range(2):
                _, kd_ = QK[d]
                tt("dve", kd_[:, i * TS:(i + 1) * TS], Kb[d], S[d], ALU.mult, [kK[d], kS[d]],
                   [("b4", 2 * d + 1)])
            for d in range(2):
                tt("dve", kdfm[i % 2][:, d, :], G[d], S[d], ALU.mult, [kG[d], kS[d]],
                   [("kdfm", i % 2, d)])
        kd_tr(ORDER[-1], 4 * (NT % 2) + 3)
        if HG_PHASES < 2:
            return

        abc3 = cap(abc[:], 0, [[9, 128], [1, 9]])
        for n, i in enumerate(ORDER):
            if n == 0:
                mset("pool", cap(XB[:], 8, [[9, 128]]), 0.0, [], [("XB", "c")])
            delta_s(i, 1)
            cpy("dve", abc3, cap(eas[:, 1, i, :], 0, [[0, 128], [1, 9]]), [("eas", 1, i)], ["abc"])
            scan(rev2d(YB[:]), rev2d(abc[:]), rev2d(XB[:]), XKEYS + ["abc"], ["YB"])
            if n + 1 < NT:
                cpy("pool", cap(XB[:], 8, [[9, 128]]), cap(YB[:], 0, [[9, 128]]), ["YB"], [("XB", "c")])
            for c in range(8):
                act(sbt[:, i * 8 + c, :], cap(YB[:], c + 1, [[9, 128]]), AF.Copy, ["YB", ("eRs", 1, i)],
                    [("sbt", i)], scale=eRs[:, 1, i, c:c + 1])
        if HG_PHASES < 3:
            return

        def c_state(i):
            p = i % 2
            gt, kgt = tmpf[3 * p], ("tmpf", 3 * p)
            t6, t7 = tmpf[6], tmpf[7]
            sf = sft[p]
            sfkey = ("sft", p)
            if i == 0:
                mset("pool", cap(XB[:], 0, [[9, 128]]), 0.0, [], [("XB", "c")])
            delta_s(i, 0)
            bz = proj_fm(slot, 4, i)
            cpy("dve", abc3, cap(eas[:, 0, i, :], 0, [[0, 128], [1, 9]]), [("eas", 0, i)], ["abc"])
            scan(YB[:], abc[:], XB[:], XKEYS + ["abc"], ["YB"])
            if i + 1 < NT:
                cpy("pool", cap(XB[:], 0, [[9, 128]]), cap(YB[:], 8, [[9, 128]]), ["YB"], [("XB", "c")])
            for c in range(8):
                act(sf[:, c, :], cap(YB[:], c, [[9, 128]]), AF.Copy, ["YB", ("eRs", 0, i)], [sfkey],
                    scale=eRs[:, 0, i, c:c + 1])
            act(t6, ps[bz][:, :], AF.Exp, [("ps", bz)], [("tmpf", 6)], scale=-1.0)
            act(t7, t6, AF.Ln, [("tmpf", 6)], [("tmpf", 7)], bias=1.0)
            act(t6, t7, AF.Exp, [("tmpf", 7)], [("tmpf", 6)], scale=-1.0)
            tt("dve", gt, ps[bz][:, :], t6, ALU.mult, [("ps", bz), ("tmpf", 6)], [kgt])

        def c_out(i):
            p = i % 2
            gt, osq, rs = tmpf[3 * p], tmpf[3 * p + 1], tmpf[3 * p + 2]
            kgt, kosq, krs = ("tmpf", 3 * p), ("tmpf", 3 * p + 1), ("tmpf", 3 * p + 2)
            sf = sft[p]
            sfkey = ("sft", p)
            bs = bank()
            psc = ps[bs][:, :].rearrange("p (d b t) -> p d b t", d=2, b=4)
            for d in range(2):
                qd, kd_ = QK[d]
                for c in range(8):
                    half = c % 2
                    lo = i * TS + c * CH
                    mm(psc[half * 64:(half + 1) * 64, d, c // 2, :], kd_[:, lo:lo + CH], qd[:, lo:lo + CH],
                       True, True, [("b4", 2 * d), ("b4", 2 * d + 1)], [("ps", bs)])
            sc = scsb[p]
            sckey = ("scsb", p)
            tt("dve", osq, ps[bs][:, :], mask2[:].rearrange("p d b t -> p (d b t)"), ALU.mult,
               [("ps", bs), "mask2"], [kosq])
            tt("dve", sc[:, 0, :, :].rearrange("p b t -> p (b t)"), osq[:, 0:256], osq[:, 256:512], ALU.add,
               [kosq], [sckey])
            bos = [bank(), bank()]
            for half in range(2):
                bo = bos[half]
                for cc in range(4):
                    c = cc * 2 + half
                    s = i * 4 + cc
                    lo = i * TS + c * CH
                    oc = ps[bo][:, cc * CH:(cc + 1) * CH]
                    vl = vsb[half * 64:(half + 1) * 64, s, :]
                    mm(oc, vl, sc[half * 64:(half + 1) * 64, 0, cc, :], True, False,
                       [("vsb", i), sckey], [("ps", bo)])
                    mm(oc, sf[:, c, :], qf_[:, lo:lo + CH], False, False, [sfkey, ("b4", 0)], [("ps", bo)])
                    mm(oc, sbt[:, i * 8 + c, :], qb_[:, lo:lo + CH], False, True, [("sbt", i), ("b4", 2)],
                       [("ps", bo)])
            bn = bank()
            for half in range(2):
                act(osq[:, half * 256:(half + 1) * 256], ps[bos[half]][:, 0:256], AF.Square,
                    [("ps", bos[half]), sckey], [kosq])
            for half in range(2):
                mm(ps[bn][:, half * 256:(half + 1) * 256], onesf[:], osq[:, half * 256:(half + 1) * 256],
                   True, True, ["onesf", kosq], [("ps", bn)])
            act(rs, ps[bn][:, :], AF.Ln, [("ps", bn)], [krs], scale=1.0 / 128.0, bias=EPS)
            act(rs, rs, AF.Exp, [krs], [krs], scale=-0.5)
            rs4 = rs.rearrange("p (h c t) -> p h c t", h=2, c=4)
            gt4 = gt.rearrange("p (c h t) -> p h c t", h=2, c=4)
            tt("pool", rs4, rs4, gt4, ALU.mult, [krs, kgt], [krs])
            yst = ystg[p]
            yst4 = yst[:].rearrange("p (c h t) -> p h c t", h=2, c=4)
            for half in range(2):
                stt(yst4[:, half, :, :], ps[bos[half]][:, 0:256].rearrange("p (c t) -> p c t", c=4), hnw_j,
                    rs4[:, half, :, :], ALU.mult, ALU.mult, [("ps", bos[half]), krs, "pb"],
                    [("ystg", p)])
            dma("sp", ys[j][:, i * TS:(i + 1) * TS], yst[:], [("ystg", p)], [("ysd", j, i)])

        c_state(0)
        for i in range(NT):
            if i + 1 < NT:
                c_state(i + 1)
            c_out(i)

    def hgrn_phase1(hl):
        w_in_l = hgrn_w_in[hl]
        slot_next = load_w(w_in_l, 0, 5)
        for j in range(HG_HEADS):
            slot = slot_next
            if j + 1 < HG_HEADS:
                slot_next = load_w(w_in_l, j + 1, 5)
            hgrn_head(hl, j, slot)

    if first:
        phase0(layers[0])
    for li, layer in enumerate(layers):
        jl = layer // 2
        is_last_layer = (li == len(layers) - 1)
        region_barrier()
        if layer % 2 == 0:
            load_wout(conv_w_out[jl])
            conv_phase1(jl)
        else:
            hgrn_phase1(jl)
            load_wout(hgrn_w_out[jl])
        region_barrier()
        xsrc = x_in if (li == 0 and first) else xs
        final = is_last_layer and last
        if final:
            nrow = final_norm_w[0:1, :]
        else:
            nrow = norm_w[layer + 1:layer + 2, :]
        phase2(xsrc, nrow, final)

    P.add("sp", lambda h: h.nop(), [("outd", s) for s in range(NSUB)], [])

    P.finalize(nc, st)
    print("ops:", {e: len(v) for e, v in P.eng_ops.items()}, "waits:", sum(len(o.waits) for o in P.ops))
    with st:
        with nc.Block() as block:
            @block.tensor
            def _(h):
                P.emit("pe", h)

            @block.scalar
            def _(h):
                P.emit("act", h)

            @block.vector
            def _(h):
                P.emit("dve", h)

            @block.gpsimd
            def _(h):
                P.emit("pool", h)

            @block.sync
            def _(h):
                P.emit("sp", h)
    return nc


_CACHE = {}


def _get_prog(key):
    if key not in _CACHE:
        _CACHE[key] = build_program(*key)
    return _CACHE[key]


def kernel(x, norm_w, final_norm_w, conv_w_in, conv_kernel, conv_w_out,
           hgrn_w_in, hgrn_lb_logits, hgrn_norm_w, hgrn_w_out):
    n = 8
    f = lambda a: np.ascontiguousarray(np.asarray(a, dtype=np.float32))
    shared = {
        "norm_w": f(norm_w), "final_norm_w": f(final_norm_w).reshape(1, D),
        "conv_w_in": f(conv_w_in), "conv_kernel": f(conv_kernel), "conv_w_out": f(conv_w_out),
        "hgrn_w_in": f(hgrn_w_in), "hgrn_lb_logits": f(hgrn_lb_logits),
        "hgrn_norm_w": f(hgrn_norm_w), "hgrn_w_out": f(hgrn_w_out),
    }
    x = f(x)
    nc = _get_prog(((0, 1, 2, 3), True, True))
    in_maps = [dict(shared, x=x[b]) for b in range(n)]
    res = run_bass_kernel_spmd(nc, in_maps, core_ids=list(range(n)))
    return np.stack([np.asarray(r["out"]) for r in res.results], axis=0).astype(np.float32)
```

```python
import contextlib
import numpy as np
import concourse.bass as bass
import concourse.mybir as mybir
from concourse.bass_utils import run_bass_kernel_spmd
from concourse.ap import AP

F32 = mybir.dt.float32
BF16 = mybir.dt.bfloat16
AF = mybir.ActivationFunctionType
ALU = mybir.AluOpType

T = 4096
D = 1024
E = 2048
NJ = 16
NT = 8
TS = 512
NSUB = 32
CH = 64
EPS = 1e-6
DEPTH = 4
EPOCH = 30000
DMA_ROT = 8
HG_HEADS = 16
HG_PHASES = 3


class Op:
    __slots__ = ("eng", "fn", "dma", "waits", "signal", "sem", "val", "eidx", "deps", "gidx")

    def __init__(self, eng, fn, dma):
        self.eng = eng
        self.fn = fn
        self.dma = dma
        self.waits = []
        self.signal = dma
        self.sem = None
        self.val = None
        self.deps = []


class Prog:
    ENGS = ("pe", "act", "dve", "pool", "sp")

    def __init__(self):
        self.ops = []
        self.eng_ops = {e: [] for e in self.ENGS}
        self.last_w = {}
        self.readers = {}

    def add(self, eng, fn, reads=(), writes=(), dma=False):
        op = Op(eng, fn, dma)
        op.gidx = len(self.ops)
        op.eidx = len(self.eng_ops[eng])
        deps = {}

        def need(p, typ):
            if p is op:
                return
            if p.dma or op.dma or p.eng != op.eng or typ == "RAW":
                deps[id(p)] = p

        for k in reads:
            p = self.last_w.get(k)
            if p is not None:
                need(p, "RAW")
        for k in writes:
            p = self.last_w.get(k)
            if p is not None:
                need(p, "WAW")
            for r in self.readers.get(k, {}).values():
                if isinstance(r, list):
                    for rr in r:
                        need(rr, "WAR")
                else:
                    need(r, "WAR")
        for k in writes:
            self.last_w[k] = op
            self.readers[k] = {}
        for k in reads:
            d = self.readers.setdefault(k, {})
            if dma:
                d.setdefault("dma", []).append(op)
            else:
                d[eng] = op
        op.deps = list(deps.values())
        self.ops.append(op)
        self.eng_ops[eng].append(op)
        return op

    def finalize(self, nc, stack):
        seen = {e: {f: -1 for f in self.ENGS} for e in self.ENGS}
        seen_dma = {e: set() for e in self.ENGS}
        for op in self.ops:
            e = op.eng
            for p in sorted(op.deps, key=lambda q: q.gidx):
                if p.dma:
                    if id(p) in seen_dma[e]:
                        continue
                    seen_dma[e].add(id(p))
                    op.waits.append(p)
                else:
                    if seen[e][p.eng] >= p.eidx:
                        continue
                    seen[e][p.eng] = p.eidx
                    p.signal = True
                    op.waits.append(p)
        for e in self.ENGS:
            cnt = 0
            sems = []
            dcnt = 0
            dsems = []
            dlast = {}
            for op in self.eng_ops[e]:
                if op.dma:
                    slot = dcnt % DMA_ROT
                    if slot >= len(dsems):
                        dsems.append(stack.enter_context(nc.semaphore("d_%s_%d" % (e, slot))))
                    prev = dlast.get(slot)
                    if prev is not None and all(w is not prev for w in op.waits):
                        op.waits.append(prev)
                    op.sem = dsems[slot]
                    op.val = 16 * (dcnt // DMA_ROT + 1)
                    dlast[slot] = op
                    dcnt += 1
                elif op.signal:
                    ep = cnt // EPOCH
                    if ep >= len(sems):
                        sems.append(stack.enter_context(nc.semaphore("c_%s_%d" % (e, ep))))
                    op.sem = sems[ep]
                    op.val = cnt % EPOCH + 1
                    cnt += 1

    def emit(self, eng, h):
        for op in self.eng_ops[eng]:
            for p in op.waits:
                h.wait_ge(p.sem, p.val)
            ins = op.fn(h)
            if op.dma:
                ins.then_inc(op.sem, 16)
            elif op.signal:
                ins.then_inc(op.sem, 1)


def cap(base, offset, dims):
    a = base.ap
    return AP(base.tensor, base.offset + offset, [list(a[0])] + [list(d) for d in dims])


def rev2d(ap):
    a = ap.ap
    assert len(a) == 2
    s, n = a[1]
    return AP(ap.tensor, ap.offset + s * (n - 1), [list(a[0]), [-s, n]])


def build_program(layers, first, last):
    nc = bass.Bass("TRN2", target_bir_lowering=False)
    P = Prog()
    st = contextlib.ExitStack()

    def din(name, shape, dt=F32):
        return nc.dram_tensor(name, list(shape), dt, kind="ExternalInput").ap()

    x_in = din("x", [T, D])
    norm_w = din("norm_w", [DEPTH, D])
    final_norm_w = din("final_norm_w", [1, D])
    conv_w_in = din("conv_w_in", [2, D, 4 * E])
    conv_kernel = din("conv_kernel", [2, 3, E])
    conv_w_out = din("conv_w_out", [2, E, D])
    hgrn_w_in = din("hgrn_w_in", [2, D, 5 * E])
    hgrn_lb_logits = din("hgrn_lb_logits", [2, E])
    hgrn_norm_w = din("hgrn_norm_w", [2, E])
    hgrn_w_out = din("hgrn_w_out", [2, E, D])
    out = nc.dram_tensor("out", [T, D], F32, kind="ExternalOutput").ap()
    xs = nc.dram_tensor("xs_scr", [T, D], F32, kind="Internal").ap()
    ys = nc.dram_tensor("ys_scr", [NJ, 128, T], BF16, kind="Internal").ap()

    def sb(name, shape, dt=F32):
        return st.enter_context(nc.sbuf_tensor(name, list(shape), dt))

    hT = sb("hT", [128, 8, T], BF16)
    big4 = sb("big4", [128, 4, T], BF16)
    wbuf = [sb("wbuf%d" % i, [128, 5, 8, 128], BF16) for i in range(2)]
    ident = sb("ident", [128, 128], BF16)
    identf = sb("identf", [128, 128], F32)
    onesf = sb("onesf", [128, 128], F32)
    par_a = sb("par_a", [96, 128], F32)
    par_b = sb("par_b", [64, 128], F32)
    ck = sb("ck", [128, 96], F32)
    pb = sb("pb", [128, 64], F32)
    lbt = sb("lbt", [128, 2, 16], F32)
    omlt = sb("omlt", [128, 2, 16], F32)
    nomlt = sb("nomlt", [128, 2, 16], F32)
    mask2 = sb("mask2", [128, 2, 4, CH], F32)
    m01 = sb("m01", [128, TS + CH], F32)
    ystg = [sb("ystg%d" % i, [128, TS], BF16) for i in range(2)]
    ssq = sb("ssq", [128, 8], F32)
    dummy = sb("dummy_t", [128, 8], F32)
    scrA = sb("scrA", [128, T + 4], F32)
    scrB = sb("scrB", [128, T], F32)
    scrC = sb("scrC", [128, T], F32)
    vbuf = scrA[:, 0:T + 2]
    kdT = scrA[:, 0:T].bitcast(BF16).rearrange("p (d s k) -> p d s k", d=2, s=NSUB)
    gbuf = scrB[:, 0:T]
    sbt = scrB[:, 0:T].bitcast(BF16).rearrange("p (c v) -> p c v", v=128)
    ytile = [scrA[:, 0:T].bitcast(BF16).rearrange("p (j t) -> p j t", j=NJ),
             scrB[:, 0:T].bitcast(BF16).rearrange("p (j t) -> p j t", j=NJ)]
    tmpf = [scrC[:, k * TS:(k + 1) * TS] for k in range(8)]
    xt = [scrC[:, 0:1024], scrC[:, 1024:2048]]
    hb = [scrC[:, 2048:2560].bitcast(BF16), scrC[:, 2560:3072].bitcast(BF16)]
    wbc = scrC[:, 3072:4096]
    vsb = sb("vsb", [128, NSUB, 128], BF16)
    sft = [sb("sft%d" % i, [128, 8, 128], BF16) for i in range(2)]
    stF = sb("stF", [128, 128 * 9], F32)
    stB = sb("stB", [128, 128 * 9], F32)
    abc = sb("abc", [128, 128 * 9], F32)
    kdfm = [sb("kdfm%d" % i, [128, 2, TS], BF16) for i in range(2)]
    scsb = [sb("scsb%d" % i, [128, 2, 4, CH], BF16) for i in range(2)]
    eas = sb("eas", [128, 2, NT, 9], F32)
    eRs = sb("eRs", [128, 2, NT, 8], F32)

    ps = [st.enter_context(nc.psum_tensor("ps%d" % i, [128, 512], F32)) for i in range(8)]
    ps_ctr = [0]

    def bank():
        b = ps_ctr[0] % 8
        ps_ctr[0] += 1
        return b

    REGIONS = ("scrA", "scrB", "scrC")

    def rtag(reads, *aps):
        reads = list(reads)
        for a in aps:
            if a is None or not hasattr(a, "tensor"):
                continue
            n = a.tensor.name
            if n in REGIONS and ("REG", n) not in reads:
                reads.append(("REG", n))
        return reads

    def act(out_, in_, func, reads, writes, scale=None, bias=None, accum=None):
        kw = {}
        if scale is not None:
            kw["scale"] = scale
        if bias is not None:
            kw["bias"] = bias
        if accum is not None:
            kw["accum_out"] = accum
        P.add("act", lambda h: h.activation(out=out_, in_=in_, func=func, **kw),
              rtag(reads, out_, in_, accum), writes)

    def mm(out_, lhsT, rhs, start, stop, reads, writes):
        P.add("pe", lambda h: h.matmul(out_, lhsT=lhsT, rhs=rhs, start=start, stop=stop),
              rtag(reads, lhsT, rhs), writes)

    def tr(out_, in_, idn, reads, writes):
        P.add("pe", lambda h: h.transpose(out_, in_, idn), rtag(reads, in_), writes)

    def dma(eng, out_, in_, reads, writes):
        P.add(eng, lambda h: h.dma_start(out=out_, in_=in_), rtag(reads, out_, in_), writes, dma=True)

    def tt(eng, out_, in0, in1, op, reads, writes):
        P.add(eng, lambda h: h.tensor_tensor(out=out_, in0=in0, in1=in1, op=op),
              rtag(reads, out_, in0, in1), writes)

    def tsc(eng, out_, in0, s1, s2, op0, op1, reads, writes):
        if op1 is None:
            P.add(eng, lambda h: h.tensor_scalar(out=out_, in0=in0, scalar1=s1, scalar2=None, op0=op0),
                  rtag(reads, out_, in0), writes)
        else:
            P.add(eng, lambda h: h.tensor_scalar(out=out_, in0=in0, scalar1=s1, scalar2=s2, op0=op0,
                                                 op1=op1), rtag(reads, out_, in0), writes)

    def stt(out_, in0, scalar, in1, op0, op1, reads, writes):
        P.add("dve", lambda h: h.scalar_tensor_tensor(out=out_, in0=in0, scalar=scalar, in1=in1,
                                                      op0=op0, op1=op1),
              rtag(reads, out_, in0, in1), writes)

    def scan(out_, d0, d1, reads, writes):
        P.add("dve", lambda h: h.tensor_tensor_scan(out=out_, data0=d0, data1=d1, initial=0.0,
                                                    op0=ALU.mult, op1=ALU.add),
              rtag(reads, out_, d0, d1), writes)

    def red(out_, in_, reads, writes):
        P.add("dve", lambda h: h.tensor_reduce(out=out_, in_=in_, axis=mybir.AxisListType.X, op=ALU.add),
              rtag(reads, out_, in_), writes)

    def cpy(eng, out_, in_, reads, writes):
        P.add(eng, lambda h: h.tensor_copy(out=out_, in_=in_), rtag(reads, out_, in_), writes)

    def mset(eng, ap, val, reads, writes):
        P.add(eng, lambda h: h.memset(ap, val), rtag(reads, ap), writes)

    def region_barrier():
        for n in REGIONS:
            P.add("pool", lambda h: h.memset(dummy[:, 0:1], 0.0), [], [("REG", n)])

    mset("pool", identf[:], 0.0, [], ["identf"])
    P.add("pool", lambda h: h.affine_select(out=identf[:], in_=identf[:], pattern=[[-1, 128]],
                                            compare_op=ALU.not_equal, fill=1.0, base=0,
                                            channel_multiplier=1), ["identf"], ["identf"])
    cpy("pool", ident[:], identf[:], ["identf"], ["ident"])
    mset("pool", onesf[:], 1.0, [], ["onesf"])
    mset("pool", m01[:], 1.0, [], ["m01"])
    mset("pool", m01[:, 0:TS + 1:CH], 0.0, ["m01"], ["m01"])
    mset("pool", eas[:], 0.0, [], [("eas", d, i) for d in range(2) for i in range(NT)])
    mset("pool", mask2[:], 1.0, [], ["mask2"])
    P.add("pool", lambda h: h.affine_select(out=mask2[0:64, 0, :, :], in_=mask2[0:64, 0, :, :],
                                            pattern=[[0, 4], [1, CH]], compare_op=ALU.is_ge, fill=0.0,
                                            base=0, channel_multiplier=-1), ["mask2"], ["mask2"])
    P.add("pool", lambda h: h.affine_select(out=mask2[0:64, 1, :, :], in_=mask2[0:64, 1, :, :],
                                            pattern=[[0, 4], [-1, CH]], compare_op=ALU.is_ge, fill=0.0,
                                            base=0, channel_multiplier=1), ["mask2"], ["mask2"])
    dma("sp", mask2[64:128, :, :, :], mask2[0:64, :, :, :], ["mask2"], ["mask2"])

    dma("sp", par_a[:], conv_kernel.rearrange("l k (j p) -> (l k j) p", p=128), [], ["par_a"])
    dma("sp", par_b[0:32, :], hgrn_lb_logits.rearrange("l (j p) -> (l j) p", p=128), [], ["par_b0"])
    dma("sp", par_b[32:64, :], hgrn_norm_w.rearrange("l (j p) -> (l j) p", p=128), [], ["par_b1"])
    b = bank()
    tr(ps[b][:, 0:96], par_a[:], identf[0:96, 0:96], ["par_a", "identf"], [("ps", b)])
    cpy("dve", ck[:], ps[b][:, 0:96], [("ps", b)], ["ck"])
    b = bank()
    tr(ps[b][:, 0:64], par_b[:], identf[0:64, 0:64], ["par_b0", "par_b1", "identf"], [("ps", b)])
    cpy("dve", pb[:], ps[b][:, 0:64], [("ps", b)], ["pb"])
    mset("dve", lbt[:], 0.0, [], ["lbt"])
    tt("dve", lbt[:, 1, :], pb[:, 16:32], pb[:, 0:16], ALU.subtract, ["pb", "lbt"], ["lbt"])
    act(lbt[:, 1, :], lbt[:, 1, :], AF.Sigmoid, ["lbt"], ["lbt"])
    tsc("dve", lbt[:], lbt[:], 0.0, 1.0 - 1e-6, ALU.max, ALU.min, ["lbt"], ["lbt"])
    tsc("dve", omlt[:], lbt[:], -1.0, 1.0, ALU.mult, ALU.add, ["lbt"], ["omlt"])
    tsc("dve", nomlt[:], lbt[:], 1.0, -1.0, ALU.mult, ALU.add, ["lbt"], ["nomlt"])

    def norm_tail(s, final):
        par = s % 2
        xn = xt[par]
        xk = ("xt", par)
        col = s % 8
        sq = ssq[:, col:col + 1]
        act(hb[par], xn, AF.Square, [xk], [("hb", par), ("ssq", col)], accum=sq)
        act(sq, sq, AF.Ln, [("ssq", col)], [("ssq", col)], scale=1.0 / D, bias=EPS)
        act(sq, sq, AF.Exp, [("ssq", col)], [("ssq", col)], scale=-0.5)
        if final:
            stt(xn, xn, sq, wbc, ALU.mult, ALU.mult, [xk, ("ssq", col), "wbc"], [xk])
            dma("sp", out[s * 128:(s + 1) * 128, :], xn, [xk], [("outd", s)])
            return
        stt(hb[par], xn, sq, wbc, ALU.mult, ALU.mult, [xk, ("ssq", col), "wbc"], [("hb", par)])
        b = bank()
        pst = ps[b][:].bitcast(BF16)
        for c in range(8):
            tr(pst[:, c * 128:(c + 1) * 128], hb[par][:, c * 128:(c + 1) * 128], ident[:],
               [("hb", par), "ident"], [("ps", b)])
        act(hT[:, :, s * 128:(s + 1) * 128], pst.rearrange("p (c t) -> p c t", c=8), AF.Copy,
            [("ps", b)], [("hT", s)])

    def load_wbc(src_row):
        dma("sp", wbc, src_row.partition_broadcast(128), [], ["wbc"])

    def phase0(layer):
        load_wbc(norm_w[layer:layer + 1, :])
        for s in range(NSUB):
            dma("sp", xt[s % 2], x_in[s * 128:(s + 1) * 128, :], [], [("xt", s % 2)])
            norm_tail(s, False)

    def load_wout(w_out_l):
        for m in range(4):
            dma("pool", big4[:, m, :].rearrange("p (j d) -> p j d", j=4),
                w_out_l[m * 512:(m + 1) * 512, :].rearrange("(j p) d -> p j d", p=128),
                [], [("b4", m)])

    def phase2(xsrc, next_w_row, final):
        load_wbc(next_w_row)
        wout = big4[:].rearrange("p m (j d) -> p (m j) d", j=4)
        for i in range(NT):
            ybt = ytile[i % 2]
            ykey = ("ytile", i % 2)
            dma("sp", ybt, ys[:, :, i * TS:(i + 1) * TS].rearrange("j p t -> p j t"),
                [("ysd", j, i) for j in range(NJ)], [ykey])
            for ss in range(4):
                s = i * 4 + ss
                par = s % 2
                x_t = xt[par]
                dma("sp", x_t, xsrc[s * 128:(s + 1) * 128, :], [("xsd", s)], [("xt", par)])
                bks = []
                for dh in range(2):
                    b = bank()
                    bks.append(b)
                    for j in range(NJ):
                        mm(ps[b][:, :], ybt[:, j, ss * 128:(ss + 1) * 128],
                           wout[:, j, dh * 512:(dh + 1) * 512], j == 0, j == NJ - 1,
                           [ykey, ("b4", j // 4)], [("ps", b)])
                for dh in range(2):
                    b = bks[dh]
                    tt("dve", x_t[:, dh * 512:(dh + 1) * 512], ps[b][:, :], x_t[:, dh * 512:(dh + 1) * 512],
                       ALU.add, [("ps", b), ("xt", par)], [("xt", par)])
                if not final:
                    dma("sp", xs[s * 128:(s + 1) * 128, :], x_t, [("xt", par)], [("xsd", s)])
                norm_tail(s, final)

    wslot = [0]

    def load_w(w_in_l, j, ngroups):
        slot = wslot[0] % 2
        wslot[0] += 1
        for g in range(ngroups):
            dma("pool", wbuf[slot][:, g, :, :],
                w_in_l[:, g * E + j * 128: g * E + (j + 1) * 128].rearrange("(c p) e -> p c e", p=128),
                [], [("w", slot, g)])
        return slot

    def proj_fm(slot, g, i):
        b = bank()
        for c in range(8):
            mm(ps[b][:, :], wbuf[slot][:, g, c, :], hT[:, c, i * TS:(i + 1) * TS], c == 0, c == 7,
               [("w", slot, g)] + [("hT", i * 4 + q) for q in range(4)], [("ps", b)])
        return b

    def conv_tile(cl, j, i):
        cv = tmpf[4 + i % 2]
        ckey = ("tmpf", 4 + i % 2)
        k0 = ck[:, cl * 48 + 0 * 16 + j: cl * 48 + 0 * 16 + j + 1]
        k1 = ck[:, cl * 48 + 1 * 16 + j: cl * 48 + 1 * 16 + j + 1]
        k2 = ck[:, cl * 48 + 2 * 16 + j: cl * 48 + 2 * 16 + j + 1]
        lo = 1 + i * TS
        vr = [("v", q) for q in (i - 1, i, i + 1) if 0 <= q < NT] + ["vpad", "ck"]
        tsc("dve", cv, vbuf[:, lo:lo + TS], k1, None, ALU.mult, None, vr, [ckey])
        stt(cv, vbuf[:, lo - 1:lo - 1 + TS], k0, cv, ALU.mult, ALU.add, vr + [ckey], [ckey])
        stt(cv, vbuf[:, lo + 1:lo + 1 + TS], k2, cv, ALU.mult, ALU.add, vr + [ckey], [ckey])
        yst = ystg[i % 2]
        tt("dve", yst[:], cv, gbuf[:, i * TS:(i + 1) * TS], ALU.mult, [ckey, ("g", i)], [("ystg", i % 2)])
        dma("sp", ys[j][:, i * TS:(i + 1) * TS], yst[:], [("ystg", i % 2)], [("ysd", j, i)])

    def conv_phase1(cl):
        w_in_l = conv_w_in[cl]
        mset("pool", vbuf[:, 0:1], 0.0, [], ["vpad"])
        mset("pool", vbuf[:, T + 1:T + 2], 0.0, ["vpad"], ["vpad"])
        slot_next = load_w(w_in_l, 0, 4)
        for j in range(NJ):
            slot = slot_next
            if j + 1 < NJ:
                slot_next = load_w(w_in_l, j + 1, 4)
            for i in range(NT):
                bb = proj_fm(slot, 0, i)
                bc = proj_fm(slot, 1, i)
                bu = proj_fm(slot, 2, i)
                bz = proj_fm(slot, 3, i)
                szt = tmpf[i % 2]
                ct = tmpf[2 + i % 2]
                act(szt, ps[bz][:, :], AF.Silu, [("ps", bz)], [("tmpf", i % 2)])
                act(ct, ps[bc][:, :], AF.Copy, [("ps", bc)], [("tmpf", 2 + i % 2)])
                lo = 1 + i * TS
                tt("dve", vbuf[:, lo:lo + TS], ps[bu][:, :], ct, ALU.mult,
                   [("ps", bu), ("tmpf", 2 + i % 2)], [("v", i)])
                tt("dve", gbuf[:, i * TS:(i + 1) * TS], ps[bb][:, :], szt, ALU.mult,
                   [("ps", bb), ("tmpf", i % 2)], [("g", i)])
                if i >= 1:
                    conv_tile(cl, j, i - 1)
            conv_tile(cl, j, NT - 1)

    QK = [(big4[:, 0, :], big4[:, 1, :]), (big4[:, 2, :], big4[:, 3, :])]
    ORDER = list(range(NT - 1, -1, -1))

    def kd_transposes(i):
        b = bank()
        pst = ps[b][:].bitcast(BF16)
        kd = kdfm[i % 2]
        for d in range(2):
            for ss in range(4):
                col = (d * 4 + ss) * 128
                tr(pst[:, col:col + 128], kd[:, d, ss * 128:(ss + 1) * 128], ident[:],
                   [("kdfm", i % 2, d), "ident"], [("ps", b)])
        act(kdT[:, :, i * 4:(i + 1) * 4, :], pst.rearrange("p (d s k) -> p d s k", d=2, s=4),
            AF.Copy, [("ps", b)], [("kdT", i)])

    dAR = sb("dAR", [128, 2, 8], F32)
    hs = sb("hs", [128, 2, 8], F32)
    ecs = sb("ecs", [128, 2, 8], F32)
    XB, YB = stF, stB

    def delta_s(i, d):
        for half in range(2):
            b = bank()
            for cc in range(4):
                c = cc * 2 + half
                s = i * 4 + cc
                mm(ps[b][:, cc * 128:(cc + 1) * 128],
                   kdT[half * 64:(half + 1) * 64, d, s, :], vsb[half * 64:(half + 1) * 64, s, :],
                   True, True, [("kdT", i), ("vsb", i)], [("ps", b)])
            off = (1 if d == 0 else 0) + half
            dst = cap(XB[:], off, [[2, 4], [9, 128]])
            src = ps[b][:, :].rearrange("p (c v) -> p c v", c=4)
            if half == 0:
                act(dst, src, AF.Copy, [("ps", b)], [("XB", "s", half)])
            else:
                cpy("dve", dst, src, [("ps", b)], [("XB", "s", half)])

    XKEYS = [("XB", "s", 0), ("XB", "s", 1), ("XB", "c")]

    def hgrn_head(hl, j, slot):
        hnw_j = pb[:, 32 + hl * 16 + j: 32 + hl * 16 + j + 1]
        lb_j = lbt[:, hl, j:j + 1]
        oml_j = omlt[:, hl, j:j + 1]
        noml_j = nomlt[:, hl, j:j + 1]
        gs = -1.0 if hl == 0 else 1.0
        qf_, kf_ = QK[0]
        qb_, kb_ = QK[1]

        def proj_to(bnk, g, i):
            for c in range(8):
                mm(ps[bnk][:, :], wbuf[slot][:, g, c, :], hT[:, c, i * TS:(i + 1) * TS], c == 0, c == 7,
                   [("w", slot, g)] + [("hT", i * 4 + q) for q in range(4)], [("ps", bnk)])

        def kd_tr_pe(i, bnk):
            pst = ps[bnk][:].bitcast(BF16)
            kd = kdfm[i % 2]
            for d in range(2):
                for ss in range(4):
                    col = (d * 4 + ss) * 128
                    tr(pst[:, col:col + 128], kd[:, d, ss * 128:(ss + 1) * 128], ident[:],
                       [("kdfm", i % 2, d), "ident"], [("ps", bnk)])

        def kd_tr_evac(i, bnk):
            pst = ps[bnk][:].bitcast(BF16)
            act(kdT[:, :, i * 4:(i + 1) * 4, :], pst.rearrange("p (d s k) -> p d s k", d=2, s=4),
                AF.Copy, [("ps", bnk)], [("kdT", i)])

        def kd_tr(i, bnk):
            kd_tr_pe(i, bnk)
            kd_tr_evac(i, bnk)

        for n, i in enumerate(ORDER):
            p = n % 2
            bff, bfb, bq, bv = 4 * p, 4 * p + 1, 4 * p + 2, 4 * p + 3
            bf = [bff, bfb]
            proj_to(bff, 1, i)
            proj_to(bfb, 2, i)
            proj_to(bq, 0, i)
            for ss in range(4):
                for c in range(8):
                    mm(ps[bv][:, ss * 128:(ss + 1) * 128],
                       hT[:, c, i * TS + ss * 128: i * TS + (ss + 1) * 128], wbuf[slot][:, 3, c, :],
                       c == 0, c == 7, [("w", slot, 3), ("hT", i * 4 + ss)], [("ps", bv)])
            if n >= 1:
                kd_tr_pe(ORDER[n - 1], 4 * (1 - p) + 3)
            S = [tmpf[0], tmpf[4]]
            L = [tmpf[1], tmpf[5]]
            Kb = [tmpf[2], tmpf[6]]
            G = [tmpf[3], tmpf[7]]
            kS = [("tmpf", 0), ("tmpf", 4)]
            kL = [("tmpf", 1), ("tmpf", 5)]
            kK = [("tmpf", 2), ("tmpf", 6)]
            kG = [("tmpf", 3), ("tmpf", 7)]
            rcol = [CH // 2 - 1, CH // 2]
            acol = [CH - 1, 0]
            for d in range(2):
                act(S[d], ps[bf[d]][:, :], AF.Exp, [("ps", bf[d])], [kS[d]], scale=-1.0)
            for d in range(2):
                act(L[d], S[d], AF.Ln, [kS[d]], [kL[d]], bias=1.0)
            for d in range(2):
                act(S[d], L[d], AF.Exp, [kL[d]], [kS[d]], scale=-1.0)
            if hl != 0:
                for d in range(2):
                    act(L[d], S[d], AF.Ln, [kS[d], "lbt", "omlt"], [kL[d]], scale=oml_j, bias=lb_j)
            for d in range(2):
                act(Kb[d], S[d], AF.Identity, [kS[d], "omlt", "nomlt"], [kK[d]], scale=noml_j, bias=oml_j)
            act(vsb[:, i * 4:(i + 1) * 4, :], ps[bv][:, :].rearrange("p (s v) -> p s v", s=4), AF.Copy,
                [("ps", bv)], [("vsb", i)])
            if n >= 1:
                kd_tr_evac(ORDER[n - 1], 4 * (1 - p) + 3)
            for d in range(2):
                L3 = L[d].rearrange("p (c t) -> p c t", t=CH)
                half_view = L3[:, :, 0:CH // 2] if d == 0 else L3[:, :, CH // 2:CH]
                red(hs[:, d, :], half_view, [kL[d]], [("hs", d)])
            for d in range(2):
                Lpos = cap(L[d], 0 if d == 0 else CH - 1, [[CH, 8]])
                tt("dve", Lpos, Lpos, hs[:, d, :], ALU.subtract, [kL[d], ("hs", d)], [kL[d]])
            scan(G[0], m01[:, 0:TS], L[0], [kL[0], "m01"], [kG[0]])
            scan(rev2d(G[1]), rev2d(m01[:, 1:TS + 1]), rev2d(L[1]), [kL[1], "m01"], [kG[1]])
            for d in range(2):
                Av = cap(G[d], acol[d], [[CH, 8]])
                tt("dve", dAR[:, d, :], Av, hs[:, d, :], ALU.add, [kG[d], ("hs", d)], [("dAR", d)])
                ea_dst = eas[:, 0, i, 1:9] if d == 0 else eas[:, 1, i, 0:8]
                act(ecs[:, d, :], Av, AF.Exp, [kG[d]], [("ecs", d)], scale=gs)
                act(eRs[:, d, i, :], hs[:, d, :], AF.Exp, [("hs", d)], [("eRs", d, i)], scale=gs)
                act(ea_dst, dAR[:, d, :], AF.Exp, [("dAR", d)], [("eas", d, i)], scale=gs)
            for d in range(2):
                act(L[d], G[d], AF.Exp, [kG[d]], [kL[d]], scale=gs)
            for d in range(2):
                qd, _ = QK[d]
                stt(qd[:, i * TS:(i + 1) * TS], ps[bq][:, :], 128.0 ** -0.5, L[d], ALU.mult, ALU.mult,
                    [("ps", bq), kL[d]], [("b4", 2 * d)])
            for d in range(2):
                act(S[d], G[d], AF.Exp, [kG[d], kK[d]], [kS[d]], scale=-gs)
            for d in range(2):
                G3 = G[d].rearrange("p (c t) -> p c t", t=CH)
                K3 = Kb[d].rearrange("p (c t) -> p c t", t=CH)
                tt("dve", G3, K3, cap(ecs[:, d, :], 0, [[1, 8], [0, CH]]), ALU.mult,
                   [kK[d], kS[d], kL[d], ("ecs", d)], [kG[d]])
            for d in range(2):
                _, kd_ = QK[d]
                tt("dve", kd_[:, i * TS:(i + 1) * TS], Kb[d], S[d], ALU.mult, [kK[d], kS[d]],
                   [("b4", 2 * d + 1)])
            for d in range(2):
                tt("dve", kdfm[i % 2][:, d, :], G[d], S[d], ALU.mult, [kG[d], kS[d]],
                   [("kdfm", i % 2, d)])
        kd_tr(ORDER[-1], 4 * (NT % 2) + 3)
        if HG_PHASES < 2:
            return

        abc3 = cap(abc[:], 0, [[9, 128], [1, 9]])
        for n, i in enumerate(ORDER):
            if n == 0:
                mset("pool", cap(XB[:], 8, [[9, 128]]), 0.0, [], [("XB", "c")])
            delta_s(i, 1)
            cpy("dve", abc3, cap(eas[:, 1, i, :], 0, [[0, 128], [1, 9]]), [("eas", 1, i)], ["abc"])
            scan(rev2d(YB[:]), rev2d(abc[:]), rev2d(XB[:]), XKEYS + ["abc"], ["YB"])
            if n + 1 < NT:
                cpy("pool", cap(XB[:], 8, [[9, 128]]), cap(YB[:], 0, [[9, 128]]), ["YB"], [("XB", "c")])
            tt("pool", sbt[:, i * 8:i * 8 + 4, :], cap(YB[:], 1, [[1, 4], [9, 128]]),
               cap(eRs[:, 1, i, :], 0, [[1, 4], [0, 128]]), ALU.mult, ["YB", ("eRs", 1, i)], [("sbt", i, 0)])
            tt("dve", sbt[:, i * 8 + 4:i * 8 + 8, :], cap(YB[:], 5, [[1, 4], [9, 128]]),
               cap(eRs[:, 1, i, :], 4, [[1, 4], [0, 128]]), ALU.mult, ["YB", ("eRs", 1, i)], [("sbt", i, 1)])
        if HG_PHASES < 3:
            return

        cst = {}

        def c_pe_state(i):
            if i == 0:
                mset("pool", cap(XB[:], 0, [[9, 128]]), 0.0, [], [("XB", "c")])
            delta_s(i, 0)
            cst[("bz", i)] = proj_fm(slot, 4, i)

        def c_silu(i):
            p = i % 2
            gt, kgt = tmpf[3 * p], ("tmpf", 3 * p)
            t6, t7 = tmpf[6], tmpf[7]
            bz = cst[("bz", i)]
            act(t6, ps[bz][:, :], AF.Exp, [("ps", bz)], [("tmpf", 6)], scale=-1.0)
            act(t7, t6, AF.Ln, [("tmpf", 6)], [("tmpf", 7)], bias=1.0)
            act(t6, t7, AF.Exp, [("tmpf", 7)], [("tmpf", 6)], scale=-1.0)
            tt("dve", gt, ps[bz][:, :], t6, ALU.mult, [("ps", bz), ("tmpf", 6)], [kgt])

        def c_scan(i):
            p = i % 2
            sf = sft[p]
            sfkey = ("sft", p)
            cpy("dve", abc3, cap(eas[:, 0, i, :], 0, [[0, 128], [1, 9]]), [("eas", 0, i)], ["abc"])
            scan(YB[:], abc[:], XB[:], XKEYS + ["abc"], ["YB"])
            if i + 1 < NT:
                cpy("pool", cap(XB[:], 0, [[9, 128]]), cap(YB[:], 8, [[9, 128]]), ["YB"], [("XB", "c")])
            tt("pool", sf[:], cap(YB[:], 0, [[1, 8], [9, 128]]),
               cap(eRs[:, 0, i, :], 0, [[1, 8], [0, 128]]), ALU.mult, ["YB", ("eRs", 0, i)], [sfkey])

        def c_scores(i):
            p = i % 2
            osq, kosq = tmpf[3 * p + 1], ("tmpf", 3 * p + 1)
            bs = bank()
            psc = ps[bs][:, :].rearrange("p (d b t) -> p d b t", d=2, b=4)
            for d in range(2):
                qd, kd_ = QK[d]
                for c in range(8):
                    half = c % 2
                    lo = i * TS + c * CH
                    mm(psc[half * 64:(half + 1) * 64, d, c // 2, :], kd_[:, lo:lo + CH], qd[:, lo:lo + CH],
                       True, True, [("b4", 2 * d), ("b4", 2 * d + 1)], [("ps", bs)])
            sc = scsb[p]
            sckey = ("scsb", p)
            tt("dve", osq, ps[bs][:, :], mask2[:].rearrange("p d b t -> p (d b t)"), ALU.mult,
               [("ps", bs), "mask2"], [kosq])
            tt("dve", sc[:, 0, :, :].rearrange("p b t -> p (b t)"), osq[:, 0:256], osq[:, 256:512], ALU.add,
               [kosq], [sckey])

        def c_o_pe(i):
            p = i % 2
            osq, rs = tmpf[3 * p + 1], tmpf[3 * p + 2]
            kosq, krs = ("tmpf", 3 * p + 1), ("tmpf", 3 * p + 2)
            sf = sft[p]
            sfkey = ("sft", p)
            sc = scsb[p]
            sckey = ("scsb", p)
            bos = [bank(), bank()]
            cst[("bos", i)] = bos
            for half in range(2):
                bo = bos[half]
                for cc in range(4):
                    c = cc * 2 + half
                    s = i * 4 + cc
                    lo = i * TS + c * CH
                    oc = ps[bo][:, cc * CH:(cc + 1) * CH]
                    vl = vsb[half * 64:(half + 1) * 64, s, :]
                    mm(oc, vl, sc[half * 64:(half + 1) * 64, 0, cc, :], True, False,
                       [("vsb", i), sckey], [("ps", bo)])
                    mm(oc, sf[:, c, :], qf_[:, lo:lo + CH], False, False, [sfkey, ("b4", 0)], [("ps", bo)])
                    mm(oc, sbt[:, i * 8 + c, :], qb_[:, lo:lo + CH], False, True,
                       [("sbt", i, 0), ("sbt", i, 1), ("b4", 2)], [("ps", bo)])
            bn = bank()
            for half in range(2):
                act(osq[:, half * 256:(half + 1) * 256], ps[bos[half]][:, 0:256], AF.Square,
                    [("ps", bos[half]), sckey], [kosq])
            for half in range(2):
                mm(ps[bn][:, half * 256:(half + 1) * 256], onesf[:], osq[:, half * 256:(half + 1) * 256],
                   True, True, ["onesf", kosq], [("ps", bn)])
            act(rs, ps[bn][:, :], AF.Ln, [("ps", bn)], [krs], scale=1.0 / 128.0, bias=EPS)
            act(rs, rs, AF.Exp, [krs], [krs], scale=-0.5)

        def c_mult(i):
            p = i % 2
            gt, rs = tmpf[3 * p], tmpf[3 * p + 2]
            kgt, krs = ("tmpf", 3 * p), ("tmpf", 3 * p + 2)
            rs4 = rs.rearrange("p (h c t) -> p h c t", h=2, c=4)
            gt4 = gt.rearrange("p (c h t) -> p h c t", h=2, c=4)
            tt("pool", rs4, rs4, gt4, ALU.mult, [krs, kgt], [krs])

        def c_fin(i):
            p = i % 2
            rs, krs = tmpf[3 * p + 2], ("tmpf", 3 * p + 2)
            rs4 = rs.rearrange("p (h c t) -> p h c t", h=2, c=4)
            bos = cst[("bos", i)]
            yst = ystg[p]
            yst4 = yst[:].rearrange("p (c h t) -> p h c t", h=2, c=4)
            for half in range(2):
                stt(yst4[:, half, :, :], ps[bos[half]][:, 0:256].rearrange("p (c t) -> p c t", c=4), hnw_j,
                    rs4[:, half, :, :], ALU.mult, ALU.mult, [("ps", bos[half]), krs, "pb"],
                    [("ystg", p)])
            dma("sp", ys[j][:, i * TS:(i + 1) * TS], yst[:], [("ystg", p)], [("ysd", j, i)])

        c_pe_state(0)
        c_silu(0)
        c_scan(0)
        for i in range(NT):
            c_scores(i)
            if i >= 1:
                c_fin(i - 1)
            if i + 1 < NT:
                c_pe_state(i + 1)
                c_silu(i + 1)
            c_o_pe(i)
            if i + 1 < NT:
                c_scan(i + 1)
            c_mult(i)
        c_fin(NT - 1)

    def hgrn_phase1(hl):
        w_in_l = hgrn_w_in[hl]
        slot_next = load_w(w_in_l, 0, 5)
        for j in range(HG_HEADS):
            slot = slot_next
            if j + 1 < HG_HEADS:
                slot_next = load_w(w_in_l, j + 1, 5)
            hgrn_head(hl, j, slot)

    if first:
        phase0(layers[0])
    for li, layer in enumerate(layers):
        jl = layer // 2
        is_last_layer = (li == len(layers) - 1)
        region_barrier()
        if layer % 2 == 0:
            load_wout(conv_w_out[jl])
            conv_phase1(jl)
        else:
            hgrn_phase1(jl)
            load_wout(hgrn_w_out[jl])
        region_barrier()
        xsrc = x_in if (li == 0 and first) else xs
        final = is_last_layer and last
        if final:
            nrow = final_norm_w[0:1, :]
        else:
            nrow = norm_w[layer + 1:layer + 2, :]
        phase2(xsrc, nrow, final)

    P.add("sp", lambda h: h.nop(), [("outd", s) for s in range(NSUB)], [])

    P.finalize(nc, st)
    print("ops:", {e: len(v) for e, v in P.eng_ops.items()}, "waits:", sum(len(o.waits) for o in P.ops))
    with st:
        with nc.Block() as block:
            @block.tensor
            def _(h):
                P.emit("pe", h)

            @block.scalar
            def _(h):
                P.emit("act", h)

            @block.vector
            def _(h):
                P.emit("dve", h)

            @block.gpsimd
            def _(h):
                P.emit("pool", h)

            @block.sync
            def _(h):
                P.emit("sp", h)
    return nc


_CACHE = {}


def _get_prog(key):
    if key not in _CACHE:
        _CACHE[key] = build_program(*key)
    return _CACHE[key]


def kernel(x, norm_w, final_norm_w, conv_w_in, conv_kernel, conv_w_out,
           hgrn_w_in, hgrn_lb_logits, hgrn_norm_w, hgrn_w_out):
    n = 8
    f = lambda a: np.ascontiguousarray(np.asarray(a, dtype=np.float32))
    shared = {
        "norm_w": f(norm_w), "final_norm_w": f(final_norm_w).reshape(1, D),
        "conv_w_in": f(conv_w_in), "conv_kernel": f(conv_kernel), "conv_w_out": f(conv_w_out),
        "hgrn_w_in": f(hgrn_w_in), "hgrn_lb_logits": f(hgrn_lb_logits),
        "hgrn_norm_w": f(hgrn_norm_w), "hgrn_w_out": f(hgrn_w_out),
    }
    x = f(x)
    nc = _get_prog(((0, 1, 2, 3), True, True))
    in_maps = [dict(shared, x=x[b]) for b in range(n)]
    res = run_bass_kernel_spmd(nc, in_maps, core_ids=list(range(n)))
    return np.stack([np.asarray(r["out"]) for r in res.results], axis=0).astype(np.float32)
```

```python
import contextlib
import numpy as np
import concourse.bass as bass
import concourse.mybir as mybir
from concourse.bass_utils import run_bass_kernel_spmd
from concourse.ap import AP

F32 = mybir.dt.float32
BF16 = mybir.dt.bfloat16
AF = mybir.ActivationFunctionType
ALU = mybir.AluOpType

T = 4096
D = 1024
E = 2048
NJ = 16
NT = 8
TS = 512
NSUB = 32
CH = 64
EPS = 1e-6
DEPTH = 4
EPOCH = 30000
DMA_ROT = 8
HG_HEADS = 16
HG_PHASES = 3


class Op:
    __slots__ = ("eng", "fn", "dma", "waits", "signal", "sem", "val", "eidx", "deps", "gidx")

    def __init__(self, eng, fn, dma):
        self.eng = eng
        self.fn = fn
        self.dma = dma
        self.waits = []
        self.signal = dma
        self.sem = None
        self.val = None
        self.deps = []


class Prog:
    ENGS = ("pe", "act", "dve", "pool", "sp")

    def __init__(self):
        self.ops = []
        self.eng_ops = {e: [] for e in self.ENGS}
        self.last_w = {}
        self.readers = {}

    def add(self, eng, fn, reads=(), writes=(), dma=False):
        op = Op(eng, fn, dma)
        op.gidx = len(self.ops)
        op.eidx = len(self.eng_ops[eng])
        deps = {}

        def need(p, typ):
            if p is op:
                return
            if p.dma or op.dma or p.eng != op.eng or typ == "RAW":
                deps[id(p)] = p

        for k in reads:
            p = self.last_w.get(k)
            if p is not None:
                need(p, "RAW")
        for k in writes:
            p = self.last_w.get(k)
            if p is not None:
                need(p, "WAW")
            for r in self.readers.get(k, {}).values():
                if isinstance(r, list):
                    for rr in r:
                        need(rr, "WAR")
                else:
                    need(r, "WAR")
        for k in writes:
            self.last_w[k] = op
            self.readers[k] = {}
        for k in reads:
            d = self.readers.setdefault(k, {})
            if dma:
                d.setdefault("dma", []).append(op)
            else:
                d[eng] = op
        op.deps = list(deps.values())
        self.ops.append(op)
        self.eng_ops[eng].append(op)
        return op

    def finalize(self, nc, stack):
        seen = {e: {f: -1 for f in self.ENGS} for e in self.ENGS}
        seen_dma = {e: set() for e in self.ENGS}
        for op in self.ops:
            e = op.eng
            for p in sorted(op.deps, key=lambda q: q.gidx):
                if p.dma:
                    if id(p) in seen_dma[e]:
                        continue
                    seen_dma[e].add(id(p))
                    op.waits.append(p)
                else:
                    if seen[e][p.eng] >= p.eidx:
                        continue
                    seen[e][p.eng] = p.eidx
                    p.signal = True
                    op.waits.append(p)
        for e in self.ENGS:
            cnt = 0
            sems = []
            dcnt = 0
            dsems = []
            dlast = {}
            for op in self.eng_ops[e]:
                if op.dma:
                    slot = dcnt % DMA_ROT
                    if slot >= len(dsems):
                        dsems.append(stack.enter_context(nc.semaphore("d_%s_%d" % (e, slot))))
                    prev = dlast.get(slot)
                    if prev is not None and all(w is not prev for w in op.waits):
                        op.waits.append(prev)
                    op.sem = dsems[slot]
                    op.val = 16 * (dcnt // DMA_ROT + 1)
                    dlast[slot] = op
                    dcnt += 1
                elif op.signal:
                    ep = cnt // EPOCH
                    if ep >= len(sems):
                        sems.append(stack.enter_context(nc.semaphore("c_%s_%d" % (e, ep))))
                    op.sem = sems[ep]
                    op.val = cnt % EPOCH + 1
                    cnt += 1

    def emit(self, eng, h):
        for op in self.eng_ops[eng]:
            for p in op.waits:
                h.wait_ge(p.sem, p.val)
            ins = op.fn(h)
            if op.dma:
                ins.then_inc(op.sem, 16)
            elif op.signal:
                ins.then_inc(op.sem, 1)


def cap(base, offset, dims):
    a = base.ap
    return AP(base.tensor, base.offset + offset, [list(a[0])] + [list(d) for d in dims])


def rev2d(ap):
    a = ap.ap
    assert len(a) == 2
    s, n = a[1]
    return AP(ap.tensor, ap.offset + s * (n - 1), [list(a[0]), [-s, n]])


def build_program(layers, first, last):
    nc = bass.Bass("TRN2", target_bir_lowering=False)
    P = Prog()
    st = contextlib.ExitStack()

    def din(name, shape, dt=F32):
        return nc.dram_tensor(name, list(shape), dt, kind="ExternalInput").ap()

    x_in = din("x", [T, D])
    norm_w = din("norm_w", [DEPTH, D])
    final_norm_w = din("final_norm_w", [1, D])
    conv_w_in = din("conv_w_in", [2, D, 4 * E])
    conv_kernel = din("conv_kernel", [2, 3, E])
    conv_w_out = din("conv_w_out", [2, E, D])
    hgrn_w_in = din("hgrn_w_in", [2, D, 5 * E])
    hgrn_lb_logits = din("hgrn_lb_logits", [2, E])
    hgrn_norm_w = din("hgrn_norm_w", [2, E])
    hgrn_w_out = din("hgrn_w_out", [2, E, D])
    out = nc.dram_tensor("out", [T, D], F32, kind="ExternalOutput").ap()
    xs = nc.dram_tensor("xs_scr", [T, D], F32, kind="Internal").ap()
    ys = nc.dram_tensor("ys_scr", [NJ, 128, T], BF16, kind="Internal").ap()

    def sb(name, shape, dt=F32):
        return st.enter_context(nc.sbuf_tensor(name, list(shape), dt))

    hT = sb("hT", [128, 8, T], BF16)
    big4 = sb("big4", [128, 4, T], BF16)
    wbuf = [sb("wbuf%d" % i, [128, 5, 8, 128], BF16) for i in range(2)]
    ident = sb("ident", [128, 128], BF16)
    identf = sb("identf", [128, 128], F32)
    onesf = sb("onesf", [128, 128], F32)
    par_a = sb("par_a", [96, 128], F32)
    par_b = sb("par_b", [64, 128], F32)
    ck = sb("ck", [128, 96], F32)
    pb = sb("pb", [128, 64], F32)
    lbt = sb("lbt", [128, 2, 16], F32)
    omlt = sb("omlt", [128, 2, 16], F32)
    nomlt = sb("nomlt", [128, 2, 16], F32)
    mask2 = sb("mask2", [128, 2, 4, CH], F32)
    m01 = sb("m01", [128, TS + CH], F32)
    ystg = [sb("ystg%d" % i, [128, TS], BF16) for i in range(2)]
    ssq = sb("ssq", [128, 8], F32)
    dummy = sb("dummy_t", [128, 8], F32)
    scrA = sb("scrA", [128, T + 4], F32)
    scrB = sb("scrB", [128, T], F32)
    scrC = sb("scrC", [128, T], F32)
    vbuf = scrA[:, 0:T + 2]
    kdT = scrA[:, 0:T].bitcast(BF16).rearrange("p (d s k) -> p d s k", d=2, s=NSUB)
    gbuf = scrB[:, 0:T]
    sbt = scrB[:, 0:T].bitcast(BF16).rearrange("p (c v) -> p c v", v=128)
    ytile = [scrA[:, 0:T].bitcast(BF16).rearrange("p (j t) -> p j t", j=NJ),
             scrB[:, 0:T].bitcast(BF16).rearrange("p (j t) -> p j t", j=NJ)]
    tmpf = [scrC[:, k * TS:(k + 1) * TS] for k in range(8)]
    xt = [scrC[:, 0:1024], scrC[:, 1024:2048]]
    hb = [scrC[:, 2048:2560].bitcast(BF16), scrC[:, 2560:3072].bitcast(BF16)]
    wbc = scrC[:, 3072:4096]
    vsb = sb("vsb", [128, NSUB, 128], BF16)
    sft = [sb("sft%d" % i, [128, 8, 128], BF16) for i in range(2)]
    stF = sb("stF", [128, 128 * 9], F32)
    stB = sb("stB", [128, 128 * 9], F32)
    abc = sb("abc", [128, 128 * 9], F32)
    kdfm = [sb("kdfm%d" % i, [128, 2, TS], BF16) for i in range(2)]
    scsb = [sb("scsb%d" % i, [128, 2, 4, CH], BF16) for i in range(2)]
    eas = sb("eas", [128, 2, NT, 9], F32)
    eRs = sb("eRs", [128, 2, NT, 8], F32)

    ps = [st.enter_context(nc.psum_tensor("ps%d" % i, [128, 512], F32)) for i in range(8)]
    ps_ctr = [0]

    def bank():
        b = ps_ctr[0] % 8
        ps_ctr[0] += 1
        return b

    REGIONS = ("scrA", "scrB", "scrC")

    def rtag(reads, *aps):
        reads = list(reads)
        for a in aps:
            if a is None or not hasattr(a, "tensor"):
                continue
            n = a.tensor.name
            if n in REGIONS and ("REG", n) not in reads:
                reads.append(("REG", n))
        return reads

    def act(out_, in_, func, reads, writes, scale=None, bias=None, accum=None):
        kw = {}
        if scale is not None:
            kw["scale"] = scale
        if bias is not None:
            kw["bias"] = bias
        if accum is not None:
            kw["accum_out"] = accum
        P.add("act", lambda h: h.activation(out=out_, in_=in_, func=func, **kw),
              rtag(reads, out_, in_, accum), writes)

    def mm(out_, lhsT, rhs, start, stop, reads, writes):
        P.add("pe", lambda h: h.matmul(out_, lhsT=lhsT, rhs=rhs, start=start, stop=stop),
              rtag(reads, lhsT, rhs), writes)

    def tr(out_, in_, idn, reads, writes):
        P.add("pe", lambda h: h.transpose(out_, in_, idn), rtag(reads, in_), writes)

    def dma(eng, out_, in_, reads, writes):
        P.add(eng, lambda h: h.dma_start(out=out_, in_=in_), rtag(reads, out_, in_), writes, dma=True)

    def tt(eng, out_, in0, in1, op, reads, writes):
        P.add(eng, lambda h: h.tensor_tensor(out=out_, in0=in0, in1=in1, op=op),
              rtag(reads, out_, in0, in1), writes)

    def tsc(eng, out_, in0, s1, s2, op0, op1, reads, writes):
        if op1 is None:
            P.add(eng, lambda h: h.tensor_scalar(out=out_, in0=in0, scalar1=s1, scalar2=None, op0=op0),
                  rtag(reads, out_, in0), writes)
        else:
            P.add(eng, lambda h: h.tensor_scalar(out=out_, in0=in0, scalar1=s1, scalar2=s2, op0=op0,
                                                 op1=op1), rtag(reads, out_, in0), writes)

    def stt(out_, in0, scalar, in1, op0, op1, reads, writes):
        P.add("dve", lambda h: h.scalar_tensor_tensor(out=out_, in0=in0, scalar=scalar, in1=in1,
                                                      op0=op0, op1=op1),
              rtag(reads, out_, in0, in1), writes)

    def scan(out_, d0, d1, reads, writes):
        P.add("dve", lambda h: h.tensor_tensor_scan(out=out_, data0=d0, data1=d1, initial=0.0,
                                                    op0=ALU.mult, op1=ALU.add),
              rtag(reads, out_, d0, d1), writes)

    def red(out_, in_, reads, writes):
        P.add("dve", lambda h: h.tensor_reduce(out=out_, in_=in_, axis=mybir.AxisListType.X, op=ALU.add),
              rtag(reads, out_, in_), writes)

    def cpy(eng, out_, in_, reads, writes):
        P.add(eng, lambda h: h.tensor_copy(out=out_, in_=in_), rtag(reads, out_, in_), writes)

    def mset(eng, ap, val, reads, writes):
        P.add(eng, lambda h: h.memset(ap, val), rtag(reads, ap), writes)

    def region_barrier():
        for n in REGIONS:
            P.add("pool", lambda h: h.memset(dummy[:, 0:1], 0.0), [], [("REG", n)])

    mset("pool", identf[:], 0.0, [], ["identf"])
    P.add("pool", lambda h: h.affine_select(out=identf[:], in_=identf[:], pattern=[[-1, 128]],
                                            compare_op=ALU.not_equal, fill=1.0, base=0,
                                            channel_multiplier=1), ["identf"], ["identf"])
    cpy("pool", ident[:], identf[:], ["identf"], ["ident"])
    mset("pool", onesf[:], 1.0, [], ["onesf"])
    mset("pool", m01[:], 1.0, [], ["m01"])
    mset("pool", m01[:, 0:TS + 1:CH], 0.0, ["m01"], ["m01"])
    mset("pool", eas[:], 0.0, [], [("eas", d, i) for d in range(2) for i in range(NT)])
    mset("pool", mask2[:], 1.0, [], ["mask2"])
    P.add("pool", lambda h: h.affine_select(out=mask2[0:64, 0, :, :], in_=mask2[0:64, 0, :, :],
                                            pattern=[[0, 4], [1, CH]], compare_op=ALU.is_ge, fill=0.0,
                                            base=0, channel_multiplier=-1), ["mask2"], ["mask2"])
    P.add("pool", lambda h: h.affine_select(out=mask2[0:64, 1, :, :], in_=mask2[0:64, 1, :, :],
                                            pattern=[[0, 4], [-1, CH]], compare_op=ALU.is_ge, fill=0.0,
                                            base=0, channel_multiplier=1), ["mask2"], ["mask2"])
    dma("sp", mask2[64:128, :, :, :], mask2[0:64, :, :, :], ["mask2"], ["mask2"])

    dma("sp", par_a[:], conv_kernel.rearrange("l k (j p) -> (l k j) p", p=128), [], ["par_a"])
    dma("sp", par_b[0:32, :], hgrn_lb_logits.rearrange("l (j p) -> (l j) p", p=128), [], ["par_b0"])
    dma("sp", par_b[32:64, :], hgrn_norm_w.rearrange("l (j p) -> (l j) p", p=128), [], ["par_b1"])
    b = bank()
    tr(ps[b][:, 0:96], par_a[:], identf[0:96, 0:96], ["par_a", "identf"], [("ps", b)])
    cpy("dve", ck[:], ps[b][:, 0:96], [("ps", b)], ["ck"])
    b = bank()
    tr(ps[b][:, 0:64], par_b[:], identf[0:64, 0:64], ["par_b0", "par_b1", "identf"], [("ps", b)])
    cpy("dve", pb[:], ps[b][:, 0:64], [("ps", b)], ["pb"])
    mset("dve", lbt[:], 0.0, [], ["lbt"])
    tt("dve", lbt[:, 1, :], pb[:, 16:32], pb[:, 0:16], ALU.subtract, ["pb", "lbt"], ["lbt"])
    act(lbt[:, 1, :], lbt[:, 1, :], AF.Sigmoid, ["lbt"], ["lbt"])
    tsc("dve", lbt[:], lbt[:], 0.0, 1.0 - 1e-6, ALU.max, ALU.min, ["lbt"], ["lbt"])
    tsc("dve", omlt[:], lbt[:], -1.0, 1.0, ALU.mult, ALU.add, ["lbt"], ["omlt"])
    tsc("dve", nomlt[:], lbt[:], 1.0, -1.0, ALU.mult, ALU.add, ["lbt"], ["nomlt"])

    def norm_tail(s, final):
        par = s % 2
        xn = xt[par]
        xk = ("xt", par)
        col = s % 8
        sq = ssq[:, col:col + 1]
        act(hb[par], xn, AF.Square, [xk], [("hb", par), ("ssq", col)], accum=sq)
        act(sq, sq, AF.Ln, [("ssq", col)], [("ssq", col)], scale=1.0 / D, bias=EPS)
        act(sq, sq, AF.Exp, [("ssq", col)], [("ssq", col)], scale=-0.5)
        if final:
            stt(xn, xn, sq, wbc, ALU.mult, ALU.mult, [xk, ("ssq", col), "wbc"], [xk])
            dma("sp", out[s * 128:(s + 1) * 128, :], xn, [xk], [("outd", s)])
            return
        stt(hb[par], xn, sq, wbc, ALU.mult, ALU.mult, [xk, ("ssq", col), "wbc"], [("hb", par)])
        b = bank()
        pst = ps[b][:].bitcast(BF16)
        for c in range(8):
            tr(pst[:, c * 128:(c + 1) * 128], hb[par][:, c * 128:(c + 1) * 128], ident[:],
               [("hb", par), "ident"], [("ps", b)])
        act(hT[:, :, s * 128:(s + 1) * 128], pst.rearrange("p (c t) -> p c t", c=8), AF.Copy,
            [("ps", b)], [("hT", s)])

    def load_wbc(src_row):
        dma("sp", wbc, src_row.partition_broadcast(128), [], ["wbc"])

    def phase0(layer):
        load_wbc(norm_w[layer:layer + 1, :])
        for s in range(NSUB):
            dma("sp", xt[s % 2], x_in[s * 128:(s + 1) * 128, :], [], [("xt", s % 2)])
            norm_tail(s, False)

    def load_wout(w_out_l):
        for m in range(4):
            dma("pool", big4[:, m, :].rearrange("p (j d) -> p j d", j=4),
                w_out_l[m * 512:(m + 1) * 512, :].rearrange("(j p) d -> p j d", p=128),
                [], [("b4", m)])

    def phase2(xsrc, next_w_row, final):
        load_wbc(next_w_row)
        wout = big4[:].rearrange("p m (j d) -> p (m j) d", j=4)
        for i in range(NT):
            ybt = ytile[i % 2]
            ykey = ("ytile", i % 2)
            dma("sp", ybt, ys[:, :, i * TS:(i + 1) * TS].rearrange("j p t -> p j t"),
                [("ysd", j, i) for j in range(NJ)], [ykey])
            for ss in range(4):
                s = i * 4 + ss
                par = s % 2
                x_t = xt[par]
                dma("sp", x_t, xsrc[s * 128:(s + 1) * 128, :], [("xsd", s)], [("xt", par)])
                bks = []
                for dh in range(2):
                    b = bank()
                    bks.append(b)
                    for j in range(NJ):
                        mm(ps[b][:, :], ybt[:, j, ss * 128:(ss + 1) * 128],
                           wout[:, j, dh * 512:(dh + 1) * 512], j == 0, j == NJ - 1,
                           [ykey, ("b4", j // 4)], [("ps", b)])
                for dh in range(2):
                    b = bks[dh]
                    tt("dve", x_t[:, dh * 512:(dh + 1) * 512], ps[b][:, :], x_t[:, dh * 512:(dh + 1) * 512],
                       ALU.add, [("ps", b), ("xt", par)], [("xt", par)])
                if not final:
                    dma("sp", xs[s * 128:(s + 1) * 128, :], x_t, [("xt", par)], [("xsd", s)])
                norm_tail(s, final)

    wslot = [0]

    def load_w(w_in_l, j, ngroups):
        slot = wslot[0] % 2
        wslot[0] += 1
        for g in range(ngroups):
            dma("pool", wbuf[slot][:, g, :, :],
                w_in_l[:, g * E + j * 128: g * E + (j + 1) * 128].rearrange("(c p) e -> p c e", p=128),
                [], [("w", slot, g)])
        return slot

    def proj_fm(slot, g, i):
        b = bank()
        for c in range(8):
            mm(ps[b][:, :], wbuf[slot][:, g, c, :], hT[:, c, i * TS:(i + 1) * TS], c == 0, c == 7,
               [("w", slot, g)] + [("hT", i * 4 + q) for q in range(4)], [("ps", b)])
        return b

    def conv_tile(cl, j, i):
        cv = tmpf[4 + i % 2]
        ckey = ("tmpf", 4 + i % 2)
        k0 = ck[:, cl * 48 + 0 * 16 + j: cl * 48 + 0 * 16 + j + 1]
        k1 = ck[:, cl * 48 + 1 * 16 + j: cl * 48 + 1 * 16 + j + 1]
        k2 = ck[:, cl * 48 + 2 * 16 + j: cl * 48 + 2 * 16 + j + 1]
        lo = 1 + i * TS
        vr = [("v", q) for q in (i - 1, i, i + 1) if 0 <= q < NT] + ["vpad", "ck"]
        tsc("dve", cv, vbuf[:, lo:lo + TS], k1, None, ALU.mult, None, vr, [ckey])
        stt(cv, vbuf[:, lo - 1:lo - 1 + TS], k0, cv, ALU.mult, ALU.add, vr + [ckey], [ckey])
        stt(cv, vbuf[:, lo + 1:lo + 1 + TS], k2, cv, ALU.mult, ALU.add, vr + [ckey], [ckey])
        yst = ystg[i % 2]
        tt("dve", yst[:], cv, gbuf[:, i * TS:(i + 1) * TS], ALU.mult, [ckey, ("g", i)], [("ystg", i % 2)])
        dma("sp", ys[j][:, i * TS:(i + 1) * TS], yst[:], [("ystg", i % 2)], [("ysd", j, i)])

    def conv_phase1(cl):
        w_in_l = conv_w_in[cl]
        mset("pool", vbuf[:, 0:1], 0.0, [], ["vpad"])
        mset("pool", vbuf[:, T + 1:T + 2], 0.0, ["vpad"], ["vpad"])
        slot_next = load_w(w_in_l, 0, 4)
        for j in range(NJ):
            slot = slot_next
            if j + 1 < NJ:
                slot_next = load_w(w_in_l, j + 1, 4)
            for i in range(NT):
                bb = proj_fm(slot, 0, i)
                bc = proj_fm(slot, 1, i)
                bu = proj_fm(slot, 2, i)
                bz = proj_fm(slot, 3, i)
                szt = tmpf[i % 2]
                ct = tmpf[2 + i % 2]
                act(szt, ps[bz][:, :], AF.Silu, [("ps", bz)], [("tmpf", i % 2)])
                act(ct, ps[bc][:, :], AF.Copy, [("ps", bc)], [("tmpf", 2 + i % 2)])
                lo = 1 + i * TS
                tt("dve", vbuf[:, lo:lo + TS], ps[bu][:, :], ct, ALU.mult,
                   [("ps", bu), ("tmpf", 2 + i % 2)], [("v", i)])
                tt("dve", gbuf[:, i * TS:(i + 1) * TS], ps[bb][:, :], szt, ALU.mult,
                   [("ps", bb), ("tmpf", i % 2)], [("g", i)])
                if i >= 1:
                    conv_tile(cl, j, i - 1)
            conv_tile(cl, j, NT - 1)

    QK = [(big4[:, 0, :], big4[:, 1, :]), (big4[:, 2, :], big4[:, 3, :])]
    ORDER = list(range(NT - 1, -1, -1))

    def kd_transposes(i):
        b = bank()
        pst = ps[b][:].bitcast(BF16)
        kd = kdfm[i % 2]
        for d in range(2):
            for ss in range(4):
                col = (d * 4 + ss) * 128
                tr(pst[:, col:col + 128], kd[:, d, ss * 128:(ss + 1) * 128], ident[:],
                   [("kdfm", i % 2, d), "ident"], [("ps", b)])
        act(kdT[:, :, i * 4:(i + 1) * 4, :], pst.rearrange("p (d s k) -> p d s k", d=2, s=4),
            AF.Copy, [("ps", b)], [("kdT", i)])

    dAR = sb("dAR", [128, 2, 8], F32)
    hs = sb("hs", [128, 2, 8], F32)
    ecs = sb("ecs", [128, 2, 8], F32)
    XB, YB = stF, stB

    def delta_s(i, d):
        for half in range(2):
            b = bank()
            for cc in range(4):
                c = cc * 2 + half
                s = i * 4 + cc
                mm(ps[b][:, cc * 128:(cc + 1) * 128],
                   kdT[half * 64:(half + 1) * 64, d, s, :], vsb[half * 64:(half + 1) * 64, s, :],
                   True, True, [("kdT", i), ("vsb", i)], [("ps", b)])
            off = (1 if d == 0 else 0) + half
            dst = cap(XB[:], off, [[2, 4], [9, 128]])
            src = ps[b][:, :].rearrange("p (c v) -> p c v", c=4)
            if half == 0:
                act(dst, src, AF.Copy, [("ps", b)], [("XB", "s", half)])
            else:
                cpy("dve", dst, src, [("ps", b)], [("XB", "s", half)])

    XKEYS = [("XB", "s", 0), ("XB", "s", 1), ("XB", "c")]

    def hgrn_head(hl, j, slot):
        hnw_j = pb[:, 32 + hl * 16 + j: 32 + hl * 16 + j + 1]
        lb_j = lbt[:, hl, j:j + 1]
        oml_j = omlt[:, hl, j:j + 1]
        noml_j = nomlt[:, hl, j:j + 1]
        gs = -1.0 if hl == 0 else 1.0
        qf_, kf_ = QK[0]
        qb_, kb_ = QK[1]

        def proj_to(bnk, g, i):
            for c in range(8):
                mm(ps[bnk][:, :], wbuf[slot][:, g, c, :], hT[:, c, i * TS:(i + 1) * TS], c == 0, c == 7,
                   [("w", slot, g)] + [("hT", i * 4 + q) for q in range(4)], [("ps", bnk)])

        def kd_tr_pe(i, bnk):
            pst = ps[bnk][:].bitcast(BF16)
            kd = kdfm[i % 2]
            for d in range(2):
                for ss in range(4):
                    col = (d * 4 + ss) * 128
                    tr(pst[:, col:col + 128], kd[:, d, ss * 128:(ss + 1) * 128], ident[:],
                       [("kdfm", i % 2, d), "ident"], [("ps", bnk)])

        def kd_tr_evac(i, bnk):
            pst = ps[bnk][:].bitcast(BF16)
            act(kdT[:, :, i * 4:(i + 1) * 4, :], pst.rearrange("p (d s k) -> p d s k", d=2, s=4),
                AF.Copy, [("ps", bnk)], [("kdT", i)])

        def kd_tr(i, bnk):
            kd_tr_pe(i, bnk)
            kd_tr_evac(i, bnk)

        for n, i in enumerate(ORDER):
            p = n % 2
            bff, bfb, bq, bv = 4 * p, 4 * p + 1, 4 * p + 2, 4 * p + 3
            bf = [bff, bfb]
            proj_to(bff, 1, i)
            proj_to(bfb, 2, i)
            proj_to(bq, 0, i)
            for ss in range(4):
                for c in range(8):
                    mm(ps[bv][:, ss * 128:(ss + 1) * 128],
                       hT[:, c, i * TS + ss * 128: i * TS + (ss + 1) * 128], wbuf[slot][:, 3, c, :],
                       c == 0, c == 7, [("w", slot, 3), ("hT", i * 4 + ss)], [("ps", bv)])
            if n >= 1:
                kd_tr_pe(ORDER[n - 1], 4 * (1 - p) + 3)
            S = [tmpf[0], tmpf[4]]
            L = [tmpf[1], tmpf[5]]
            Kb = [tmpf[2], tmpf[6]]
            G = [tmpf[3], tmpf[7]]
            kS = [("tmpf", 0), ("tmpf", 4)]
            kL = [("tmpf", 1), ("tmpf", 5)]
            kK = [("tmpf", 2), ("tmpf", 6)]
            kG = [("tmpf", 3), ("tmpf", 7)]
            rcol = [CH // 2 - 1, CH // 2]
            acol = [CH - 1, 0]
            for d in range(2):
                act(S[d], ps[bf[d]][:, :], AF.Exp, [("ps", bf[d])], [kS[d]], scale=-1.0)
            for d in range(2):
                act(L[d], S[d], AF.Ln, [kS[d]], [kL[d]], bias=1.0)
            for d in range(2):
                act(S[d], L[d], AF.Exp, [kL[d]], [kS[d]], scale=-1.0)
            if hl != 0:
                for d in range(2):
                    act(L[d], S[d], AF.Ln, [kS[d], "lbt", "omlt"], [kL[d]], scale=oml_j, bias=lb_j)
            for d in range(2):
                act(Kb[d], S[d], AF.Identity, [kS[d], "omlt", "nomlt"], [kK[d]], scale=noml_j, bias=oml_j)
            act(vsb[:, i * 4:(i + 1) * 4, :], ps[bv][:, :].rearrange("p (s v) -> p s v", s=4), AF.Copy,
                [("ps", bv)], [("vsb", i)])
            if n >= 1:
                kd_tr_evac(ORDER[n - 1], 4 * (1 - p) + 3)
            for d in range(2):
                L3 = L[d].rearrange("p (c t) -> p c t", t=CH)
                half_view = L3[:, :, 0:CH // 2] if d == 0 else L3[:, :, CH // 2:CH]
                red(hs[:, d, :], half_view, [kL[d]], [("hs", d)])
            for d in range(2):
                Lpos = cap(L[d], 0 if d == 0 else CH - 1, [[CH, 8]])
                tt("dve", Lpos, Lpos, hs[:, d, :], ALU.subtract, [kL[d], ("hs", d)], [kL[d]])
            scan(G[0], m01[:, 0:TS], L[0], [kL[0], "m01"], [kG[0]])
            scan(rev2d(G[1]), rev2d(m01[:, 1:TS + 1]), rev2d(L[1]), [kL[1], "m01"], [kG[1]])
            for d in range(2):
                Av = cap(G[d], acol[d], [[CH, 8]])
                tt("dve", dAR[:, d, :], Av, hs[:, d, :], ALU.add, [kG[d], ("hs", d)], [("dAR", d)])
                ea_dst = eas[:, 0, i, 1:9] if d == 0 else eas[:, 1, i, 0:8]
                act(ecs[:, d, :], Av, AF.Exp, [kG[d]], [("ecs", d)], scale=gs)
                act(eRs[:, d, i, :], hs[:, d, :], AF.Exp, [("hs", d)], [("eRs", d, i)], scale=gs)
                act(ea_dst, dAR[:, d, :], AF.Exp, [("dAR", d)], [("eas", d, i)], scale=gs)
            for d in range(2):
                S3 = S[d].rearrange("p (c t) -> p c t", t=CH)
                K3 = Kb[d].rearrange("p (c t) -> p c t", t=CH)
                tt("pool", S3, K3, cap(ecs[:, d, :], 0, [[1, 8], [0, CH]]), ALU.mult,
                   [kK[d], ("ecs", d)], [kS[d]])
            for d in range(2):
                act(L[d], G[d], AF.Exp, [kG[d]], [kL[d]], scale=gs)
            for d in range(2):
                qd, _ = QK[d]
                stt(qd[:, i * TS:(i + 1) * TS], ps[bq][:, :], 128.0 ** -0.5, L[d], ALU.mult, ALU.mult,
                    [("ps", bq), kL[d]], [("b4", 2 * d)])
            for d in range(2):
                act(G[d], G[d], AF.Exp, [kG[d]], [kG[d]], scale=-gs)
            for d in range(2):
                _, kd_ = QK[d]
                tt("dve", kd_[:, i * TS:(i + 1) * TS], Kb[d], G[d], ALU.mult, [kK[d], kG[d]],
                   [("b4", 2 * d + 1)])
            for d in range(2):
                tt("dve", kdfm[i % 2][:, d, :], S[d], G[d], ALU.mult, [kG[d], kS[d]],
                   [("kdfm", i % 2, d)])
        kd_tr(ORDER[-1], 4 * (NT % 2) + 3)
        if HG_PHASES < 2:
            return

        abc3 = cap(abc[:], 0, [[9, 128], [1, 9]])
        for n, i in enumerate(ORDER):
            if n == 0:
                mset("pool", cap(XB[:], 8, [[9, 128]]), 0.0, [], [("XB", "c")])
            delta_s(i, 1)
            cpy("dve", abc3, cap(eas[:, 1, i, :], 0, [[0, 128], [1, 9]]), [("eas", 1, i)], ["abc"])
            scan(rev2d(YB[:]), rev2d(abc[:]), rev2d(XB[:]), XKEYS + ["abc"], ["YB"])
            if n + 1 < NT:
                cpy("pool", cap(XB[:], 8, [[9, 128]]), cap(YB[:], 0, [[9, 128]]), ["YB"], [("XB", "c")])
            tt("pool", sbt[:, i * 8:i * 8 + 4, :], cap(YB[:], 1, [[1, 4], [9, 128]]),
               cap(eRs[:, 1, i, :], 0, [[1, 4], [0, 128]]), ALU.mult, ["YB", ("eRs", 1, i)], [("sbt", i, 0)])
            tt("dve", sbt[:, i * 8 + 4:i * 8 + 8, :], cap(YB[:], 5, [[1, 4], [9, 128]]),
               cap(eRs[:, 1, i, :], 4, [[1, 4], [0, 128]]), ALU.mult, ["YB", ("eRs", 1, i)], [("sbt", i, 1)])
        if HG_PHASES < 3:
            return

        B_SC, B_DS, B_ZN = 4, (5, 6), 7

        def c_pe_state(i):
            if i == 0:
                mset("pool", cap(XB[:], 0, [[9, 128]]), 0.0, [], [("XB", "c")])
            for half in range(2):
                b = B_DS[half]
                for cc in range(4):
                    c = cc * 2 + half
                    s = i * 4 + cc
                    mm(ps[b][:, cc * 128:(cc + 1) * 128],
                       kdT[half * 64:(half + 1) * 64, 0, s, :], vsb[half * 64:(half + 1) * 64, s, :],
                       True, True, [("kdT", i), ("vsb", i)], [("ps", b)])
                dst = cap(XB[:], 1 + half, [[2, 4], [9, 128]])
                src = ps[b][:, :].rearrange("p (c v) -> p c v", c=4)
                if half == 0:
                    act(dst, src, AF.Copy, [("ps", b)], [("XB", "s", half)])
                else:
                    cpy("dve", dst, src, [("ps", b)], [("XB", "s", half)])
            for c in range(8):
                mm(ps[B_ZN][:, :], wbuf[slot][:, 4, c, :], hT[:, c, i * TS:(i + 1) * TS], c == 0, c == 7,
                   [("w", slot, 4)] + [("hT", i * 4 + q) for q in range(4)], [("ps", B_ZN)])

        def c_silu(i):
            p = i % 2
            gt, kgt = tmpf[3 * p], ("tmpf", 3 * p)
            t6, t7 = tmpf[6], tmpf[7]
            bz = B_ZN
            act(t6, ps[bz][:, :], AF.Exp, [("ps", bz)], [("tmpf", 6)], scale=-1.0)
            act(t7, t6, AF.Ln, [("tmpf", 6)], [("tmpf", 7)], bias=1.0)
            act(t6, t7, AF.Exp, [("tmpf", 7)], [("tmpf", 6)], scale=-1.0)
            tt("dve", gt, ps[bz][:, :], t6, ALU.mult, [("ps", bz), ("tmpf", 6)], [kgt])

        def c_scan_dve(i):
            cpy("dve", abc3, cap(eas[:, 0, i, :], 0, [[0, 128], [1, 9]]), [("eas", 0, i)], ["abc"])
            scan(YB[:], abc[:], XB[:], XKEYS + ["abc"], ["YB"])

        def c_scan_tail(i):
            p = i % 2
            sf = sft[p]
            if i + 1 < NT:
                cpy("pool", cap(XB[:], 0, [[9, 128]]), cap(YB[:], 8, [[9, 128]]), ["YB"], [("XB", "c")])
            tt("pool", sf[:, 0:4, :], cap(YB[:], 0, [[1, 4], [9, 128]]),
               cap(eRs[:, 0, i, :], 0, [[1, 4], [0, 128]]), ALU.mult, ["YB", ("eRs", 0, i)], [("sft", p, 0)])
            tt("dve", sf[:, 4:8, :], cap(YB[:], 4, [[1, 4], [9, 128]]),
               cap(eRs[:, 0, i, :], 4, [[1, 4], [0, 128]]), ALU.mult, ["YB", ("eRs", 0, i)], [("sft", p, 1)])

        def c_scores(i):
            p = i % 2
            osq, kosq = tmpf[3 * p + 1], ("tmpf", 3 * p + 1)
            bs = B_SC
            psc = ps[bs][:, :].rearrange("p (d b t) -> p d b t", d=2, b=4)
            for d in range(2):
                qd, kd_ = QK[d]
                for c in range(8):
                    half = c % 2
                    lo = i * TS + c * CH
                    mm(psc[half * 64:(half + 1) * 64, d, c // 2, :], kd_[:, lo:lo + CH], qd[:, lo:lo + CH],
                       True, True, [("b4", 2 * d), ("b4", 2 * d + 1)], [("ps", bs)])
            sc = scsb[p]
            sckey = ("scsb", p)
            tt("dve", osq, ps[bs][:, :], mask2[:].rearrange("p d b t -> p (d b t)"), ALU.mult,
               [("ps", bs), "mask2"], [kosq])
            tt("dve", sc[:, 0, :, :].rearrange("p b t -> p (b t)"), osq[:, 0:256], osq[:, 256:512], ALU.add,
               [kosq], [sckey])

        def c_o_pe(i):
            p = i % 2
            osq, rs = tmpf[3 * p + 1], tmpf[3 * p + 2]
            kosq, krs = ("tmpf", 3 * p + 1), ("tmpf", 3 * p + 2)
            sf = sft[p]
            sc = scsb[p]
            sckey = ("scsb", p)
            bos = [2 * p, 2 * p + 1]
            for half in range(2):
                bo = bos[half]
                for cc in range(4):
                    c = cc * 2 + half
                    s = i * 4 + cc
                    lo = i * TS + c * CH
                    oc = ps[bo][:, cc * CH:(cc + 1) * CH]
                    vl = vsb[half * 64:(half + 1) * 64, s, :]
                    mm(oc, vl, sc[half * 64:(half + 1) * 64, 0, cc, :], True, False,
                       [("vsb", i), sckey], [("ps", bo)])
                    mm(oc, sf[:, c, :], qf_[:, lo:lo + CH], False, False,
                       [("sft", p, 0), ("sft", p, 1), ("b4", 0)], [("ps", bo)])
                    mm(oc, sbt[:, i * 8 + c, :], qb_[:, lo:lo + CH], False, True,
                       [("sbt", i, 0), ("sbt", i, 1), ("b4", 2)], [("ps", bo)])
            bn = B_ZN
            for half in range(2):
                act(osq[:, half * 256:(half + 1) * 256], ps[bos[half]][:, 0:256], AF.Square,
                    [("ps", bos[half]), sckey], [kosq])
            for half in range(2):
                mm(ps[bn][:, half * 256:(half + 1) * 256], onesf[:], osq[:, half * 256:(half + 1) * 256],
                   True, True, ["onesf", kosq], [("ps", bn)])
            act(rs, ps[bn][:, :], AF.Ln, [("ps", bn)], [krs], scale=1.0 / 128.0, bias=EPS)
            act(rs, rs, AF.Exp, [krs], [krs], scale=-0.5)

        def c_mult(i):
            p = i % 2
            gt, rs = tmpf[3 * p], tmpf[3 * p + 2]
            kgt, krs = ("tmpf", 3 * p), ("tmpf", 3 * p + 2)
            rs4 = rs.rearrange("p (h c t) -> p h c t", h=2, c=4)
            gt4 = gt.rearrange("p (c h t) -> p h c t", h=2, c=4)
            tt("pool", rs4, rs4, gt4, ALU.mult, [krs, kgt], [krs])

        def c_fin(i):
            p = i % 2
            gt, rs = tmpf[3 * p], tmpf[3 * p + 2]
            kgt, krs = ("tmpf", 3 * p), ("tmpf", 3 * p + 2)
            rs4 = rs.rearrange("p (h c t) -> p h c t", h=2, c=4)
            bos = [2 * p, 2 * p + 1]
            yst = ystg[p]
            yst4 = yst[:].rearrange("p (c h t) -> p h c t", h=2, c=4)
            for half in range(2):
                stt(yst4[:, half, :, :], ps[bos[half]][:, 0:256].rearrange("p (c t) -> p c t", c=4), hnw_j,
                    rs4[:, half, :, :], ALU.mult, ALU.mult, [("ps", bos[half]), krs, "pb"],
                    [("ystg", p)])
            dma("sp", ys[j][:, i * TS:(i + 1) * TS], yst[:], [("ystg", p)], [("ysd", j, i)])

        c_pe_state(0)
        c_silu(0)
        c_scan_dve(0)
        c_scan_tail(0)
        for i in range(NT):
            c_scores(i)
            if i + 1 < NT:
                c_pe_state(i + 1)
                c_silu(i + 1)
                c_scan_dve(i + 1)
            if i >= 1:
                c_fin(i - 1)
            c_o_pe(i)
            if i + 1 < NT:
                c_scan_tail(i + 1)
            c_mult(i)
        c_fin(NT - 1)

    def hgrn_phase1(hl):
        w_in_l = hgrn_w_in[hl]
        slot_next = load_w(w_in_l, 0, 5)
        for j in range(HG_HEADS):
            slot = slot_next
            if j + 1 < HG_HEADS:
                slot_next = load_w(w_in_l, j + 1, 5)
            hgrn_head(hl, j, slot)

    if first:
        phase0(layers[0])
    for li, layer in enumerate(layers):
        jl = layer // 2
        is_last_layer = (li == len(layers) - 1)
        region_barrier()
        if layer % 2 == 0:
            load_wout(conv_w_out[jl])
            conv_phase1(jl)
        else:
            hgrn_phase1(jl)
            load_wout(hgrn_w_out[jl])
        region_barrier()
        xsrc = x_in if (li == 0 and first) else xs
        final = is_last_layer and last
        if final:
            nrow = final_norm_w[0:1, :]
        else:
            nrow = norm_w[layer + 1:layer + 2, :]
        phase2(xsrc, nrow, final)

    P.add("sp", lambda h: h.nop(), [("outd", s) for s in range(NSUB)], [])

    P.finalize(nc, st)
    print("ops:", {e: len(v) for e, v in P.eng_ops.items()}, "waits:", sum(len(o.waits) for o in P.ops))
    with st:
        with nc.Block() as block:
            @block.tensor
            def _(h):
                P.emit("pe", h)

            @block.scalar
            def _(h):
                P.emit("act", h)

            @block.vector
            def _(h):
                P.emit("dve", h)

            @block.gpsimd
            def _(h):
                P.emit("pool", h)

            @block.sync
            def _(h):
                P.emit("sp", h)
    return nc


_CACHE = {}


def _get_prog(key):
    if key not in _CACHE:
        _CACHE[key] = build_program(*key)
    return _CACHE[key]


def kernel(x, norm_w, final_norm_w, conv_w_in, conv_kernel, conv_w_out,
           hgrn_w_in, hgrn_lb_logits, hgrn_norm_w, hgrn_w_out):
    n = 8
    f = lambda a: np.ascontiguousarray(np.asarray(a, dtype=np.float32))
    shared = {
        "norm_w": f(norm_w), "final_norm_w": f(final_norm_w).reshape(1, D),
        "conv_w_in": f(conv_w_in), "conv_kernel": f(conv_kernel), "conv_w_out": f(conv_w_out),
        "hgrn_w_in": f(hgrn_w_in), "hgrn_lb_logits": f(hgrn_lb_logits),
        "hgrn_norm_w": f(hgrn_norm_w), "hgrn_w_out": f(hgrn_w_out),
    }
    x = f(x)
    nc = _get_prog(((0, 1, 2, 3), True, True))
    in_maps = [dict(shared, x=x[b]) for b in range(n)]
    res = run_bass_kernel_spmd(nc, in_maps, core_ids=list(range(n)))
    return np.stack([np.asarray(r["out"]) for r in res.results], axis=0).astype(np.float32)
```

```python
import contextlib
import numpy as np
import concourse.bass as bass
import concourse.mybir as mybir
from concourse.bass_utils import run_bass_kernel_spmd
from concourse.ap import AP

F32 = mybir.dt.float32
BF16 = mybir.dt.bfloat16
AF = mybir.ActivationFunctionType
ALU = mybir.AluOpType

T = 4096
D = 1024
E = 2048
NJ = 16
NT = 8
TS = 512
NSUB = 32
CH = 64
EPS = 1e-6
DEPTH = 4
EPOCH = 30000
DMA_ROT = 8
HG_HEADS = 16
HG_PHASES = 3


class Op:
    __slots__ = ("eng", "fn", "dma", "waits", "signal", "sem", "val", "eidx", "deps", "gidx")

    def __init__(self, eng, fn, dma):
        self.eng = eng
        self.fn = fn
        self.dma = dma
        self.waits = []
        self.signal = dma
        self.sem = None
        self.val = None
        self.deps = []


class Prog:
    ENGS = ("pe", "act", "dve", "pool", "sp")

    def __init__(self):
        self.ops = []
        self.eng_ops = {e: [] for e in self.ENGS}
        self.last_w = {}
        self.readers = {}

    def add(self, eng, fn, reads=(), writes=(), dma=False):
        op = Op(eng, fn, dma)
        op.gidx = len(self.ops)
        op.eidx = len(self.eng_ops[eng])
        deps = {}

        def need(p, typ):
            if p is op:
                return
            if p.dma or op.dma or p.eng != op.eng or typ == "RAW":
                deps[id(p)] = p

        for k in reads:
            p = self.last_w.get(k)
            if p is not None:
                need(p, "RAW")
        for k in writes:
            p = self.last_w.get(k)
            if p is not None:
                need(p, "WAW")
            for r in self.readers.get(k, {}).values():
                if isinstance(r, list):
                    for rr in r:
                        need(rr, "WAR")
                else:
                    need(r, "WAR")
        for k in writes:
            self.last_w[k] = op
            self.readers[k] = {}
        for k in reads:
            d = self.readers.setdefault(k, {})
            if dma:
                d.setdefault("dma", []).append(op)
            else:
                d[eng] = op
        op.deps = list(deps.values())
        self.ops.append(op)
        self.eng_ops[eng].append(op)
        return op

    def finalize(self, nc, stack):
        seen = {e: {f: -1 for f in self.ENGS} for e in self.ENGS}
        seen_dma = {e: set() for e in self.ENGS}
        for op in self.ops:
            e = op.eng
            for p in sorted(op.deps, key=lambda q: q.gidx):
                if p.dma:
                    if id(p) in seen_dma[e]:
                        continue
                    seen_dma[e].add(id(p))
                    op.waits.append(p)
                else:
                    if seen[e][p.eng] >= p.eidx:
                        continue
                    seen[e][p.eng] = p.eidx
                    p.signal = True
                    op.waits.append(p)
        for e in self.ENGS:
            cnt = 0
            sems = []
            dcnt = 0
            dsems = []
            dlast = {}
            for op in self.eng_ops[e]:
                if op.dma:
                    slot = dcnt % DMA_ROT
                    if slot >= len(dsems):
                        dsems.append(stack.enter_context(nc.semaphore("d_%s_%d" % (e, slot))))
                    prev = dlast.get(slot)
                    if prev is not None and all(w is not prev for w in op.waits):
                        op.waits.append(prev)
                    op.sem = dsems[slot]
                    op.val = 16 * (dcnt // DMA_ROT + 1)
                    dlast[slot] = op
                    dcnt += 1
                elif op.signal:
                    ep = cnt // EPOCH
                    if ep >= len(sems):
                        sems.append(stack.enter_context(nc.semaphore("c_%s_%d" % (e, ep))))
                    op.sem = sems[ep]
                    op.val = cnt % EPOCH + 1
                    cnt += 1

    def emit(self, eng, h):
        for op in self.eng_ops[eng]:
            for p in op.waits:
                h.wait_ge(p.sem, p.val)
            ins = op.fn(h)
            if op.dma:
                ins.then_inc(op.sem, 16)
            elif op.signal:
                ins.then_inc(op.sem, 1)


def cap(base, offset, dims):
    a = base.ap
    return AP(base.tensor, base.offset + offset, [list(a[0])] + [list(d) for d in dims])


def rev2d(ap):
    a = ap.ap
    assert len(a) == 2
    s, n = a[1]
    return AP(ap.tensor, ap.offset + s * (n - 1), [list(a[0]), [-s, n]])


def build_program(layers, first, last):
    nc = bass.Bass("TRN2", target_bir_lowering=False)
    P = Prog()
    st = contextlib.ExitStack()

    def din(name, shape, dt=F32):
        return nc.dram_tensor(name, list(shape), dt, kind="ExternalInput").ap()

    x_in = din("x", [T, D])
    norm_w = din("norm_w", [DEPTH, D])
    final_norm_w = din("final_norm_w", [1, D])
    conv_w_in = din("conv_w_in", [2, D, 4 * E])
    conv_kernel = din("conv_kernel", [2, 3, E])
    conv_w_out = din("conv_w_out", [2, E, D])
    hgrn_w_in = din("hgrn_w_in", [2, D, 5 * E])
    hgrn_lb_logits = din("hgrn_lb_logits", [2, E])
    hgrn_norm_w = din("hgrn_norm_w", [2, E])
    hgrn_w_out = din("hgrn_w_out", [2, E, D])
    out = nc.dram_tensor("out", [T, D], F32, kind="ExternalOutput").ap()
    xs = nc.dram_tensor("xs_scr", [T, D], F32, kind="Internal").ap()
    ys = nc.dram_tensor("ys_scr", [NJ, 128, T], BF16, kind="Internal").ap()

    def sb(name, shape, dt=F32):
        return st.enter_context(nc.sbuf_tensor(name, list(shape), dt))

    hT = sb("hT", [128, 8, T], BF16)
    big4 = sb("big4", [128, 4, T], BF16)
    wbuf = [sb("wbuf%d" % i, [128, 5, 8, 128], BF16) for i in range(2)]
    ident = sb("ident", [128, 128], BF16)
    identf = sb("identf", [128, 128], F32)
    onesf = sb("onesf", [128, 128], F32)
    par_a = sb("par_a", [96, 128], F32)
    par_b = sb("par_b", [64, 128], F32)
    ck = sb("ck", [128, 96], F32)
    pb = sb("pb", [128, 64], F32)
    lbt = sb("lbt", [128, 2, 16], F32)
    omlt = sb("omlt", [128, 2, 16], F32)
    nomlt = sb("nomlt", [128, 2, 16], F32)
    mask2 = sb("mask2", [128, 2, 4, CH], F32)
    m01 = sb("m01", [128, TS + CH], F32)
    ystg = [sb("ystg%d" % i, [128, TS], BF16) for i in range(2)]
    ssq = sb("ssq", [128, 8], F32)
    dummy = sb("dummy_t", [128, 8], F32)
    scrA = sb("scrA", [128, T + 4], F32)
    scrB = sb("scrB", [128, T], F32)
    scrC = sb("scrC", [128, T], F32)
    vbuf = scrA[:, 0:T + 2]
    kdT = scrA[:, 0:T].bitcast(BF16).rearrange("p (d s k) -> p d s k", d=2, s=NSUB)
    gbuf = scrB[:, 0:T]
    sbt = scrB[:, 0:T].bitcast(BF16).rearrange("p (c v) -> p c v", v=128)
    ytile = [scrA[:, 0:T].bitcast(BF16).rearrange("p (j t) -> p j t", j=NJ),
             scrB[:, 0:T].bitcast(BF16).rearrange("p (j t) -> p j t", j=NJ)]
    tmpf = [scrC[:, k * TS:(k + 1) * TS] for k in range(8)]
    xt = [scrC[:, 0:1024], scrC[:, 1024:2048]]
    hb = [scrC[:, 2048:2560].bitcast(BF16), scrC[:, 2560:3072].bitcast(BF16)]
    wbc = scrC[:, 3072:4096]
    vsb = sb("vsb", [128, NSUB, 128], BF16)
    sft = [sb("sft%d" % i, [128, 8, 128], BF16) for i in range(2)]
    stF = sb("stF", [128, 128 * 9], F32)
    stB = sb("stB", [128, 128 * 9], F32)
    abc = sb("abc", [128, 128 * 9], F32)
    kdfm = [sb("kdfm%d" % i, [128, 2, TS], BF16) for i in range(2)]
    scsb = [sb("scsb%d" % i, [128, 2, 4, CH], BF16) for i in range(2)]
    eas = sb("eas", [128, 2, NT, 9], F32)
    eRs = sb("eRs", [128, 2, NT, 8], F32)

    ps = [st.enter_context(nc.psum_tensor("ps%d" % i, [128, 512], F32)) for i in range(8)]
    ps_ctr = [0]

    def bank():
        b = ps_ctr[0] % 8
        ps_ctr[0] += 1
        return b

    REGIONS = ("scrA", "scrB", "scrC")

    def rtag(reads, *aps):
        reads = list(reads)
        for a in aps:
            if a is None or not hasattr(a, "tensor"):
                continue
            n = a.tensor.name
            if n in REGIONS and ("REG", n) not in reads:
                reads.append(("REG", n))
        return reads

    def act(out_, in_, func, reads, writes, scale=None, bias=None, accum=None):
        kw = {}
        if scale is not None:
            kw["scale"] = scale
        if bias is not None:
            kw["bias"] = bias
        if accum is not None:
            kw["accum_out"] = accum
        P.add("act", lambda h: h.activation(out=out_, in_=in_, func=func, **kw),
              rtag(reads, out_, in_, accum), writes)

    def mm(out_, lhsT, rhs, start, stop, reads, writes):
        P.add("pe", lambda h: h.matmul(out_, lhsT=lhsT, rhs=rhs, start=start, stop=stop),
              rtag(reads, lhsT, rhs), writes)

    def tr(out_, in_, idn, reads, writes):
        P.add("pe", lambda h: h.transpose(out_, in_, idn), rtag(reads, in_), writes)

    def dma(eng, out_, in_, reads, writes):
        P.add(eng, lambda h: h.dma_start(out=out_, in_=in_), rtag(reads, out_, in_), writes, dma=True)

    def tt(eng, out_, in0, in1, op, reads, writes):
        P.add(eng, lambda h: h.tensor_tensor(out=out_, in0=in0, in1=in1, op=op),
              rtag(reads, out_, in0, in1), writes)

    def tsc(eng, out_, in0, s1, s2, op0, op1, reads, writes):
        if op1 is None:
            P.add(eng, lambda h: h.tensor_scalar(out=out_, in0=in0, scalar1=s1, scalar2=None, op0=op0),
                  rtag(reads, out_, in0), writes)
        else:
            P.add(eng, lambda h: h.tensor_scalar(out=out_, in0=in0, scalar1=s1, scalar2=s2, op0=op0,
                                                 op1=op1), rtag(reads, out_, in0), writes)

    def stt(out_, in0, scalar, in1, op0, op1, reads, writes):
        P.add("dve", lambda h: h.scalar_tensor_tensor(out=out_, in0=in0, scalar=scalar, in1=in1,
                                                      op0=op0, op1=op1),
              rtag(reads, out_, in0, in1), writes)

    def scan(out_, d0, d1, reads, writes):
        P.add("dve", lambda h: h.tensor_tensor_scan(out=out_, data0=d0, data1=d1, initial=0.0,
                                                    op0=ALU.mult, op1=ALU.add),
              rtag(reads, out_, d0, d1), writes)

    def red(out_, in_, reads, writes):
        P.add("dve", lambda h: h.tensor_reduce(out=out_, in_=in_, axis=mybir.AxisListType.X, op=ALU.add),
              rtag(reads, out_, in_), writes)

    def cpy(eng, out_, in_, reads, writes):
        P.add(eng, lambda h: h.tensor_copy(out=out_, in_=in_), rtag(reads, out_, in_), writes)

    def mset(eng, ap, val, reads, writes):
        P.add(eng, lambda h: h.memset(ap, val), rtag(reads, ap), writes)

    def region_barrier():
        for n in REGIONS:
            P.add("pool", lambda h: h.memset(dummy[:, 0:1], 0.0), [], [("REG", n)])

    mset("pool", identf[:], 0.0, [], ["identf"])
    P.add("pool", lambda h: h.affine_select(out=identf[:], in_=identf[:], pattern=[[-1, 128]],
                                            compare_op=ALU.not_equal, fill=1.0, base=0,
                                            channel_multiplier=1), ["identf"], ["identf"])
    cpy("pool", ident[:], identf[:], ["identf"], ["ident"])
    mset("pool", onesf[:], 1.0, [], ["onesf"])
    mset("pool", m01[:], 1.0, [], ["m01"])
    mset("pool", m01[:, 0:TS + 1:CH], 0.0, ["m01"], ["m01"])
    mset("pool", eas[:], 0.0, [], [("eas", d, i) for d in range(2) for i in range(NT)])
    mset("pool", mask2[:], 1.0, [], ["mask2"])
    P.add("pool", lambda h: h.affine_select(out=mask2[0:64, 0, :, :], in_=mask2[0:64, 0, :, :],
                                            pattern=[[0, 4], [1, CH]], compare_op=ALU.is_ge, fill=0.0,
                                            base=0, channel_multiplier=-1), ["mask2"], ["mask2"])
    P.add("pool", lambda h: h.affine_select(out=mask2[0:64, 1, :, :], in_=mask2[0:64, 1, :, :],
                                            pattern=[[0, 4], [-1, CH]], compare_op=ALU.is_ge, fill=0.0,
                                            base=0, channel_multiplier=1), ["mask2"], ["mask2"])
    dma("sp", mask2[64:128, :, :, :], mask2[0:64, :, :, :], ["mask2"], ["mask2"])

    dma("sp", par_a[:], conv_kernel.rearrange("l k (j p) -> (l k j) p", p=128), [], ["par_a"])
    dma("sp", par_b[0:32, :], hgrn_lb_logits.rearrange("l (j p) -> (l j) p", p=128), [], ["par_b0"])
    dma("sp", par_b[32:64, :], hgrn_norm_w.rearrange("l (j p) -> (l j) p", p=128), [], ["par_b1"])
    b = bank()
    tr(ps[b][:, 0:96], par_a[:], identf[0:96, 0:96], ["par_a", "identf"], [("ps", b)])
    cpy("dve", ck[:], ps[b][:, 0:96], [("ps", b)], ["ck"])
    b = bank()
    tr(ps[b][:, 0:64], par_b[:], identf[0:64, 0:64], ["par_b0", "par_b1", "identf"], [("ps", b)])
    cpy("dve", pb[:], ps[b][:, 0:64], [("ps", b)], ["pb"])
    mset("dve", lbt[:], 0.0, [], ["lbt"])
    tt("dve", lbt[:, 1, :], pb[:, 16:32], pb[:, 0:16], ALU.subtract, ["pb", "lbt"], ["lbt"])
    act(lbt[:, 1, :], lbt[:, 1, :], AF.Sigmoid, ["lbt"], ["lbt"])
    tsc("dve", lbt[:], lbt[:], 0.0, 1.0 - 1e-6, ALU.max, ALU.min, ["lbt"], ["lbt"])
    tsc("dve", omlt[:], lbt[:], -1.0, 1.0, ALU.mult, ALU.add, ["lbt"], ["omlt"])
    tsc("dve", nomlt[:], lbt[:], 1.0, -1.0, ALU.mult, ALU.add, ["lbt"], ["nomlt"])

    def norm_tail(s, final):
        par = s % 2
        xn = xt[par]
        xk = ("xt", par)
        col = s % 8
        sq = ssq[:, col:col + 1]
        act(hb[par], xn, AF.Square, [xk], [("hb", par), ("ssq", col)], accum=sq)
        act(sq, sq, AF.Ln, [("ssq", col)], [("ssq", col)], scale=1.0 / D, bias=EPS)
        act(sq, sq, AF.Exp, [("ssq", col)], [("ssq", col)], scale=-0.5)
        if final:
            stt(xn, xn, sq, wbc, ALU.mult, ALU.mult, [xk, ("ssq", col), "wbc"], [xk])
            dma("pool", out[s * 128:(s + 1) * 128, :], xn, [xk], [("outd", s)])
            return
        stt(hb[par], xn, sq, wbc, ALU.mult, ALU.mult, [xk, ("ssq", col), "wbc"], [("hb", par)])
        b = bank()
        pst = ps[b][:].bitcast(BF16)
        for c in range(8):
            tr(pst[:, c * 128:(c + 1) * 128], hb[par][:, c * 128:(c + 1) * 128], ident[:],
               [("hb", par), "ident"], [("ps", b)])
        act(hT[:, :, s * 128:(s + 1) * 128], pst.rearrange("p (c t) -> p c t", c=8), AF.Copy,
            [("ps", b)], [("hT", s)])

    def load_wbc(src_row):
        dma("sp", wbc, src_row.partition_broadcast(128), [], ["wbc"])

    def phase0(layer):
        load_wbc(norm_w[layer:layer + 1, :])
        for s in range(NSUB):
            dma("sp", xt[s % 2], x_in[s * 128:(s + 1) * 128, :], [], [("xt", s % 2)])
            norm_tail(s, False)

    def load_wout(w_out_l):
        for m in range(4):
            dma("pool", big4[:, m, :].rearrange("p (j d) -> p j d", j=4),
                w_out_l[m * 512:(m + 1) * 512, :].rearrange("(j p) d -> p j d", p=128),
                [], [("b4", m)])

    def phase2(xsrc, next_w_row, final):
        load_wbc(next_w_row)
        wout = big4[:].rearrange("p m (j d) -> p (m j) d", j=4)
        for i in range(NT):
            ybt = ytile[i % 2]
            ykey = ("ytile", i % 2)
            dma("sp", ybt, ys[:, :, i * TS:(i + 1) * TS].rearrange("j p t -> p j t"),
                [("ysd", j, i) for j in range(NJ)], [ykey])
            for ss in range(4):
                s = i * 4 + ss
                par = s % 2
                x_t = xt[par]
                dma("sp", x_t, xsrc[s * 128:(s + 1) * 128, :], [("xsd", s)], [("xt", par)])
                bks = []
                for dh in range(2):
                    b = bank()
                    bks.append(b)
                    for j in range(NJ):
                        mm(ps[b][:, :], ybt[:, j, ss * 128:(ss + 1) * 128],
                           wout[:, j, dh * 512:(dh + 1) * 512], j == 0, j == NJ - 1,
                           [ykey, ("b4", j // 4)], [("ps", b)])
                for dh in range(2):
                    b = bks[dh]
                    tt("dve", x_t[:, dh * 512:(dh + 1) * 512], ps[b][:, :], x_t[:, dh * 512:(dh + 1) * 512],
                       ALU.add, [("ps", b), ("xt", par)], [("xt", par)])
                if not final:
                    dma("pool", xs[s * 128:(s + 1) * 128, :], x_t, [("xt", par)], [("xsd", s)])
                norm_tail(s, final)

    wslot = [0]

    def load_w(w_in_l, j, ngroups):
        slot = wslot[0] % 2
        wslot[0] += 1
        for g in range(ngroups):
            dma("pool", wbuf[slot][:, g, :, :],
                w_in_l[:, g * E + j * 128: g * E + (j + 1) * 128].rearrange("(c p) e -> p c e", p=128),
                [], [("w", slot, g)])
        return slot

    def proj_fm(slot, g, i):
        b = bank()
        for c in range(8):
            mm(ps[b][:, :], wbuf[slot][:, g, c, :], hT[:, c, i * TS:(i + 1) * TS], c == 0, c == 7,
               [("w", slot, g)] + [("hT", i * 4 + q) for q in range(4)], [("ps", b)])
        return b

    def conv_tile(cl, j, i):
        cv = tmpf[4 + i % 2]
        ckey = ("tmpf", 4 + i % 2)
        k0 = ck[:, cl * 48 + 0 * 16 + j: cl * 48 + 0 * 16 + j + 1]
        k1 = ck[:, cl * 48 + 1 * 16 + j: cl * 48 + 1 * 16 + j + 1]
        k2 = ck[:, cl * 48 + 2 * 16 + j: cl * 48 + 2 * 16 + j + 1]
        lo = 1 + i * TS
        vr = [("v", q) for q in (i - 1, i, i + 1) if 0 <= q < NT] + ["vpad", "ck"]
        tsc("dve", cv, vbuf[:, lo:lo + TS], k1, None, ALU.mult, None, vr, [ckey])
        stt(cv, vbuf[:, lo - 1:lo - 1 + TS], k0, cv, ALU.mult, ALU.add, vr + [ckey], [ckey])
        stt(cv, vbuf[:, lo + 1:lo + 1 + TS], k2, cv, ALU.mult, ALU.add, vr + [ckey], [ckey])
        yst = ystg[i % 2]
        tt("dve", yst[:], cv, gbuf[:, i * TS:(i + 1) * TS], ALU.mult, [ckey, ("g", i)], [("ystg", i % 2)])
        dma("sp", ys[j][:, i * TS:(i + 1) * TS], yst[:], [("ystg", i % 2)], [("ysd", j, i)])

    def conv_phase1(cl):
        w_in_l = conv_w_in[cl]
        mset("pool", vbuf[:, 0:1], 0.0, [], ["vpad"])
        mset("pool", vbuf[:, T + 1:T + 2], 0.0, ["vpad"], ["vpad"])
        slot_next = load_w(w_in_l, 0, 4)
        for j in range(NJ):
            slot = slot_next
            if j + 1 < NJ:
                slot_next = load_w(w_in_l, j + 1, 4)
            for i in range(NT):
                bb = proj_fm(slot, 0, i)
                bc = proj_fm(slot, 1, i)
                bu = proj_fm(slot, 2, i)
                bz = proj_fm(slot, 3, i)
                szt = tmpf[i % 2]
                ct = tmpf[2 + i % 2]
                act(szt, ps[bz][:, :], AF.Silu, [("ps", bz)], [("tmpf", i % 2)])
                act(ct, ps[bc][:, :], AF.Copy, [("ps", bc)], [("tmpf", 2 + i % 2)])
                lo = 1 + i * TS
                tt("dve", vbuf[:, lo:lo + TS], ps[bu][:, :], ct, ALU.mult,
                   [("ps", bu), ("tmpf", 2 + i % 2)], [("v", i)])
                tt("dve", gbuf[:, i * TS:(i + 1) * TS], ps[bb][:, :], szt, ALU.mult,
                   [("ps", bb), ("tmpf", i % 2)], [("g", i)])
                if i >= 1:
                    conv_tile(cl, j, i - 1)
            conv_tile(cl, j, NT - 1)

    QK = [(big4[:, 0, :], big4[:, 1, :]), (big4[:, 2, :], big4[:, 3, :])]
    ORDER = list(range(NT - 1, -1, -1))

    def kd_transposes(i):
        b = bank()
        pst = ps[b][:].bitcast(BF16)
        kd = kdfm[i % 2]
        for d in range(2):
            for ss in range(4):
                col = (d * 4 + ss) * 128
                tr(pst[:, col:col + 128], kd[:, d, ss * 128:(ss + 1) * 128], ident[:],
                   [("kdfm", i % 2, d), "ident"], [("ps", b)])
        act(kdT[:, :, i * 4:(i + 1) * 4, :], pst.rearrange("p (d s k) -> p d s k", d=2, s=4),
            AF.Copy, [("ps", b)], [("kdT", i)])

    dAR = sb("dAR", [128, 2, 8], F32)
    hs = sb("hs", [128, 2, 8], F32)
    ecs = sb("ecs", [128, 2, 8], F32)
    XB, YB = stF, stB

    def delta_s(i, d):
        for half in range(2):
            b = bank()
            for cc in range(4):
                c = cc * 2 + half
                s = i * 4 + cc
                mm(ps[b][:, cc * 128:(cc + 1) * 128],
                   kdT[half * 64:(half + 1) * 64, d, s, :], vsb[half * 64:(half + 1) * 64, s, :],
                   True, True, [("kdT", i), ("vsb", i)], [("ps", b)])
            off = (1 if d == 0 else 0) + half
            dst = cap(XB[:], off, [[2, 4], [9, 128]])
            src = ps[b][:, :].rearrange("p (c v) -> p c v", c=4)
            if half == 0:
                act(dst, src, AF.Copy, [("ps", b)], [("XB", "s", half)])
            else:
                cpy("dve", dst, src, [("ps", b)], [("XB", "s", half)])

    XKEYS = [("XB", "s", 0), ("XB", "s", 1), ("XB", "c")]

    def hgrn_head(hl, j, slot):
        hnw_j = pb[:, 32 + hl * 16 + j: 32 + hl * 16 + j + 1]
        lb_j = lbt[:, hl, j:j + 1]
        oml_j = omlt[:, hl, j:j + 1]
        noml_j = nomlt[:, hl, j:j + 1]
        gs = -1.0 if hl == 0 else 1.0
        qf_, kf_ = QK[0]
        qb_, kb_ = QK[1]

        def proj_to(bnk, g, i):
            for c in range(8):
                mm(ps[bnk][:, :], wbuf[slot][:, g, c, :], hT[:, c, i * TS:(i + 1) * TS], c == 0, c == 7,
                   [("w", slot, g)] + [("hT", i * 4 + q) for q in range(4)], [("ps", bnk)])

        def kd_tr_pe(i, bnk):
            pst = ps[bnk][:].bitcast(BF16)
            kd = kdfm[i % 2]
            for d in range(2):
                for ss in range(4):
                    col = (d * 4 + ss) * 128
                    tr(pst[:, col:col + 128], kd[:, d, ss * 128:(ss + 1) * 128], ident[:],
                       [("kdfm", i % 2, d), "ident"], [("ps", bnk)])

        def kd_tr_evac(i, bnk):
            pst = ps[bnk][:].bitcast(BF16)
            act(kdT[:, :, i * 4:(i + 1) * 4, :], pst.rearrange("p (d s k) -> p d s k", d=2, s=4),
                AF.Copy, [("ps", bnk)], [("kdT", i)])

        def kd_tr(i, bnk):
            kd_tr_pe(i, bnk)
            kd_tr_evac(i, bnk)

        for n, i in enumerate(ORDER):
            p = n % 2
            bff, bfb, bq, bv = 4 * p, 4 * p + 1, 4 * p + 2, 4 * p + 3
            bf = [bff, bfb]
            proj_to(bff, 1, i)
            proj_to(bfb, 2, i)
            proj_to(bq, 0, i)
            for ss in range(4):
                for c in range(8):
                    mm(ps[bv][:, ss * 128:(ss + 1) * 128],
                       hT[:, c, i * TS + ss * 128: i * TS + (ss + 1) * 128], wbuf[slot][:, 3, c, :],
                       c == 0, c == 7, [("w", slot, 3), ("hT", i * 4 + ss)], [("ps", bv)])
            if n >= 1:
                kd_tr_pe(ORDER[n - 1], 4 * (1 - p) + 3)
            S = [tmpf[0], tmpf[4]]
            L = [tmpf[1], tmpf[5]]
            Kb = [tmpf[2], tmpf[6]]
            G = [tmpf[3], tmpf[7]]
            kS = [("tmpf", 0), ("tmpf", 4)]
            kL = [("tmpf", 1), ("tmpf", 5)]
            kK = [("tmpf", 2), ("tmpf", 6)]
            kG = [("tmpf", 3), ("tmpf", 7)]
            rcol = [CH // 2 - 1, CH // 2]
            acol = [CH - 1, 0]
            for d in range(2):
                act(S[d], ps[bf[d]][:, :], AF.Exp, [("ps", bf[d])], [kS[d]], scale=-1.0)
            for d in range(2):
                act(L[d], S[d], AF.Ln, [kS[d]], [kL[d]], bias=1.0)
            for d in range(2):
                act(S[d], L[d], AF.Exp, [kL[d]], [kS[d]], scale=-1.0)
            if hl != 0:
                for d in range(2):
                    act(L[d], S[d], AF.Ln, [kS[d], "lbt", "omlt"], [kL[d]], scale=oml_j, bias=lb_j)
            for d in range(2):
                act(Kb[d], S[d], AF.Identity, [kS[d], "omlt", "nomlt"], [kK[d]], scale=noml_j, bias=oml_j)
            act(vsb[:, i * 4:(i + 1) * 4, :], ps[bv][:, :].rearrange("p (s v) -> p s v", s=4), AF.Copy,
                [("ps", bv)], [("vsb", i)])
            if n >= 1:
                kd_tr_evac(ORDER[n - 1], 4 * (1 - p) + 3)
            for d in range(2):
                L3 = L[d].rearrange("p (c t) -> p c t", t=CH)
                half_view = L3[:, :, 0:CH // 2] if d == 0 else L3[:, :, CH // 2:CH]
                red(hs[:, d, :], half_view, [kL[d]], [("hs", d)])
            for d in range(2):
                Lpos = cap(L[d], 0 if d == 0 else CH - 1, [[CH, 8]])
                tt("dve", Lpos, Lpos, hs[:, d, :], ALU.subtract, [kL[d], ("hs", d)], [kL[d]])
            scan(G[0], m01[:, 0:TS], L[0], [kL[0], "m01"], [kG[0]])
            scan(rev2d(G[1]), rev2d(m01[:, 1:TS + 1]), rev2d(L[1]), [kL[1], "m01"], [kG[1]])
            for d in range(2):
                Av = cap(G[d], acol[d], [[CH, 8]])
                tt("dve", dAR[:, d, :], Av, hs[:, d, :], ALU.add, [kG[d], ("hs", d)], [("dAR", d)])
                ea_dst = eas[:, 0, i, 1:9] if d == 0 else eas[:, 1, i, 0:8]
                act(ecs[:, d, :], Av, AF.Exp, [kG[d]], [("ecs", d)], scale=gs)
                act(eRs[:, d, i, :], hs[:, d, :], AF.Exp, [("hs", d)], [("eRs", d, i)], scale=gs)
                act(ea_dst, dAR[:, d, :], AF.Exp, [("dAR", d)], [("eas", d, i)], scale=gs)
            for d in range(2):
                S3 = S[d].rearrange("p (c t) -> p c t", t=CH)
                K3 = Kb[d].rearrange("p (c t) -> p c t", t=CH)
                tt("pool", S3, K3, cap(ecs[:, d, :], 0, [[1, 8], [0, CH]]), ALU.mult,
                   [kK[d], ("ecs", d)], [kS[d]])
            for d in range(2):
                act(L[d], G[d], AF.Exp, [kG[d]], [kL[d]], scale=gs)
            for d in range(2):
                qd, _ = QK[d]
                stt(qd[:, i * TS:(i + 1) * TS], ps[bq][:, :], 128.0 ** -0.5, L[d], ALU.mult, ALU.mult,
                    [("ps", bq), kL[d]], [("b4", 2 * d)])
            for d in range(2):
                act(G[d], G[d], AF.Exp, [kG[d]], [kG[d]], scale=-gs)
            for d in range(2):
                _, kd_ = QK[d]
                tt("dve", kd_[:, i * TS:(i + 1) * TS], Kb[d], G[d], ALU.mult, [kK[d], kG[d]],
                   [("b4", 2 * d + 1)])
            for d in range(2):
                tt("dve", kdfm[i % 2][:, d, :], S[d], G[d], ALU.mult, [kG[d], kS[d]],
                   [("kdfm", i % 2, d)])
        kd_tr(ORDER[-1], 4 * (NT % 2) + 3)
        if HG_PHASES < 2:
            return

        abc3 = cap(abc[:], 0, [[9, 128], [1, 9]])
        for n, i in enumerate(ORDER):
            if n == 0:
                mset("pool", cap(XB[:], 8, [[9, 128]]), 0.0, [], [("XB", "c")])
            delta_s(i, 1)
            cpy("dve", abc3, cap(eas[:, 1, i, :], 0, [[0, 128], [1, 9]]), [("eas", 1, i)], ["abc"])
            scan(rev2d(YB[:]), rev2d(abc[:]), rev2d(XB[:]), XKEYS + ["abc"], ["YB"])
            if n + 1 < NT:
                cpy("pool", cap(XB[:], 8, [[9, 128]]), cap(YB[:], 0, [[9, 128]]), ["YB"], [("XB", "c")])
            tt("pool", sbt[:, i * 8:i * 8 + 4, :], cap(YB[:], 1, [[1, 4], [9, 128]]),
               cap(eRs[:, 1, i, :], 0, [[1, 4], [0, 128]]), ALU.mult, ["YB", ("eRs", 1, i)], [("sbt", i, 0)])
            tt("dve", sbt[:, i * 8 + 4:i * 8 + 8, :], cap(YB[:], 5, [[1, 4], [9, 128]]),
               cap(eRs[:, 1, i, :], 4, [[1, 4], [0, 128]]), ALU.mult, ["YB", ("eRs", 1, i)], [("sbt", i, 1)])
        if HG_PHASES < 3:
            return

        B_SC, B_DS, B_ZN = 4, (5, 6), 7

        def c_pe_state(i):
            if i == 0:
                mset("pool", cap(XB[:], 0, [[9, 128]]), 0.0, [], [("XB", "c")])
            for half in range(2):
                b = B_DS[half]
                for cc in range(4):
                    c = cc * 2 + half
                    s = i * 4 + cc
                    mm(ps[b][:, cc * 128:(cc + 1) * 128],
                       kdT[half * 64:(half + 1) * 64, 0, s, :], vsb[half * 64:(half + 1) * 64, s, :],
                       True, True, [("kdT", i), ("vsb", i)], [("ps", b)])
                dst = cap(XB[:], 1 + half, [[2, 4], [9, 128]])
                src = ps[b][:, :].rearrange("p (c v) -> p c v", c=4)
                if half == 0:
                    act(dst, src, AF.Copy, [("ps", b)], [("XB", "s", half)])
                else:
                    cpy("dve", dst, src, [("ps", b)], [("XB", "s", half)])
            for c in range(8):
                mm(ps[B_ZN][:, :], wbuf[slot][:, 4, c, :], hT[:, c, i * TS:(i + 1) * TS], c == 0, c == 7,
                   [("w", slot, 4)] + [("hT", i * 4 + q) for q in range(4)], [("ps", B_ZN)])

        def c_silu(i):
            p = i % 2
            gt, kgt = tmpf[3 * p], ("tmpf", 3 * p)
            t6, t7 = tmpf[6], tmpf[7]
            bz = B_ZN
            act(t6, ps[bz][:, :], AF.Exp, [("ps", bz)], [("tmpf", 6)], scale=-1.0)
            act(t7, t6, AF.Ln, [("tmpf", 6)], [("tmpf", 7)], bias=1.0)
            act(t6, t7, AF.Exp, [("tmpf", 7)], [("tmpf", 6)], scale=-1.0)
            tt("dve", gt, ps[bz][:, :], t6, ALU.mult, [("ps", bz), ("tmpf", 6)], [kgt])

        def c_scan_dve(i):
            cpy("dve", abc3, cap(eas[:, 0, i, :], 0, [[0, 128], [1, 9]]), [("eas", 0, i)], ["abc"])
            scan(YB[:], abc[:], XB[:], XKEYS + ["abc"], ["YB"])

        def c_scan_tail(i):
            p = i % 2
            sf = sft[p]
            if i + 1 < NT:
                cpy("pool", cap(XB[:], 0, [[9, 128]]), cap(YB[:], 8, [[9, 128]]), ["YB"], [("XB", "c")])
            tt("pool", sf[:, 0:4, :], cap(YB[:], 0, [[1, 4], [9, 128]]),
               cap(eRs[:, 0, i, :], 0, [[1, 4], [0, 128]]), ALU.mult, ["YB", ("eRs", 0, i)], [("sft", p, 0)])
            tt("dve", sf[:, 4:8, :], cap(YB[:], 4, [[1, 4], [9, 128]]),
               cap(eRs[:, 0, i, :], 4, [[1, 4], [0, 128]]), ALU.mult, ["YB", ("eRs", 0, i)], [("sft", p, 1)])

        def c_scores(i):
            p = i % 2
            osq, kosq = tmpf[3 * p + 1], ("tmpf", 3 * p + 1)
            bs = B_SC
            psc = ps[bs][:, :].rearrange("p (d b t) -> p d b t", d=2, b=4)
            for d in range(2):
                qd, kd_ = QK[d]
                for c in range(8):
                    half = c % 2
                    lo = i * TS + c * CH
                    mm(psc[half * 64:(half + 1) * 64, d, c // 2, :], kd_[:, lo:lo + CH], qd[:, lo:lo + CH],
                       True, True, [("b4", 2 * d), ("b4", 2 * d + 1)], [("ps", bs)])
            sc = scsb[p]
            sckey = ("scsb", p)
            tt("dve", osq, ps[bs][:, :], mask2[:].rearrange("p d b t -> p (d b t)"), ALU.mult,
               [("ps", bs), "mask2"], [kosq])
            tt("dve", sc[:, 0, :, :].rearrange("p b t -> p (b t)"), osq[:, 0:256], osq[:, 256:512], ALU.add,
               [kosq], [sckey])

        def c_o_pe(i):
            p = i % 2
            osq, rs = tmpf[3 * p + 1], tmpf[3 * p + 2]
            kosq, krs = ("tmpf", 3 * p + 1), ("tmpf", 3 * p + 2)
            sf = sft[p]
            sc = scsb[p]
            sckey = ("scsb", p)
            bos = [2 * p, 2 * p + 1]
            for half in range(2):
                bo = bos[half]
                for cc in range(4):
                    c = cc * 2 + half
                    s = i * 4 + cc
                    lo = i * TS + c * CH
                    oc = ps[bo][:, cc * CH:(cc + 1) * CH]
                    vl = vsb[half * 64:(half + 1) * 64, s, :]
                    mm(oc, vl, sc[half * 64:(half + 1) * 64, 0, cc, :], True, False,
                       [("vsb", i), sckey], [("ps", bo)])
                    mm(oc, sf[:, c, :], qf_[:, lo:lo + CH], False, False,
                       [("sft", p, 0), ("sft", p, 1), ("b4", 0)], [("ps", bo)])
                    mm(oc, sbt[:, i * 8 + c, :], qb_[:, lo:lo + CH], False, True,
                       [("sbt", i, 0), ("sbt", i, 1), ("b4", 2)], [("ps", bo)])
            bn = B_ZN
            for half in range(2):
                act(osq[:, half * 256:(half + 1) * 256], ps[bos[half]][:, 0:256], AF.Square,
                    [("ps", bos[half]), sckey], [kosq])
            for half in range(2):
                mm(ps[bn][:, half * 256:(half + 1) * 256], onesf[:], osq[:, half * 256:(half + 1) * 256],
                   True, True, ["onesf", kosq], [("ps", bn)])
            act(rs, ps[bn][:, :], AF.Ln, [("ps", bn)], [krs], scale=1.0 / 128.0, bias=EPS)
            act(rs, rs, AF.Exp, [krs], [krs], scale=-0.5)

        def c_mult(i):
            p = i % 2
            gt, rs = tmpf[3 * p], tmpf[3 * p + 2]
            kgt, krs = ("tmpf", 3 * p), ("tmpf", 3 * p + 2)
            rs4 = rs.rearrange("p (h c t) -> p h c t", h=2, c=4)
            gt4 = gt.rearrange("p (c h t) -> p h c t", h=2, c=4)
            tt("pool", rs4, rs4, gt4, ALU.mult, [krs, kgt], [krs])

        def c_fin(i):
            p = i % 2
            gt, rs = tmpf[3 * p], tmpf[3 * p + 2]
            kgt, krs = ("tmpf", 3 * p), ("tmpf", 3 * p + 2)
            rs4 = rs.rearrange("p (h c t) -> p h c t", h=2, c=4)
            bos = [2 * p, 2 * p + 1]
            yst = ystg[p]
            yst4 = yst[:].rearrange("p (c h t) -> p h c t", h=2, c=4)
            for half in range(2):
                stt(yst4[:, half, :, :], ps[bos[half]][:, 0:256].rearrange("p (c t) -> p c t", c=4), hnw_j,
                    rs4[:, half, :, :], ALU.mult, ALU.mult, [("ps", bos[half]), krs, "pb"],
                    [("ystg", p)])
            dma("sp", ys[j][:, i * TS:(i + 1) * TS], yst[:], [("ystg", p)], [("ysd", j, i)])

        c_pe_state(0)
        c_silu(0)
        c_scan_dve(0)
        c_scan_tail(0)
        for i in range(NT):
            c_scores(i)
            if i + 1 < NT:
                c_pe_state(i + 1)
                c_silu(i + 1)
                c_scan_dve(i + 1)
            if i >= 1:
                c_fin(i - 1)
            c_o_pe(i)
            if i + 1 < NT:
                c_scan_tail(i + 1)
            c_mult(i)
        c_fin(NT - 1)

    def hgrn_phase1(hl):
        w_in_l = hgrn_w_in[hl]
        slot_next = load_w(w_in_l, 0, 5)
        for j in range(HG_HEADS):
            slot = slot_next
            if j + 1 < HG_HEADS:
                slot_next = load_w(w_in_l, j + 1, 5)
            hgrn_head(hl, j, slot)

    if first:
        phase0(layers[0])
    for li, layer in enumerate(layers):
        jl = layer // 2
        is_last_layer = (li == len(layers) - 1)
        region_barrier()
        if layer % 2 == 0:
            load_wout(conv_w_out[jl])
            conv_phase1(jl)
        else:
            hgrn_phase1(jl)
            load_wout(hgrn_w_out[jl])
        region_barrier()
        xsrc = x_in if (li == 0 and first) else xs
        final = is_last_layer and last
        if final:
            nrow = final_norm_w[0:1, :]
        else:
            nrow = norm_w[layer + 1:layer + 2, :]
        phase2(xsrc, nrow, final)

    P.add("sp", lambda h: h.nop(), [("outd", s) for s in range(NSUB)], [])

    P.finalize(nc, st)
    print("ops:", {e: len(v) for e, v in P.eng_ops.items()}, "waits:", sum(len(o.waits) for o in P.ops))
    with st:
        with nc.Block() as block:
            @block.tensor
            def _(h):
                P.emit("pe", h)

            @block.scalar
            def _(h):
                P.emit("act", h)

            @block.vector
            def _(h):
                P.emit("dve", h)

            @block.gpsimd
            def _(h):
                P.emit("pool", h)

            @block.sync
            def _(h):
                P.emit("sp", h)
    return nc


_CACHE = {}


def _get_prog(key):
    if key not in _CACHE:
        _CACHE[key] = build_program(*key)
    return _CACHE[key]


def kernel(x, norm_w, final_norm_w, conv_w_in, conv_kernel, conv_w_out,
           hgrn_w_in, hgrn_lb_logits, hgrn_norm_w, hgrn_w_out):
    n = 8
    f = lambda a: np.ascontiguousarray(np.asarray(a, dtype=np.float32))
    shared = {
        "norm_w": f(norm_w), "final_norm_w": f(final_norm_w).reshape(1, D),
        "conv_w_in": f(conv_w_in), "conv_kernel": f(conv_kernel), "conv_w_out": f(conv_w_out),
        "hgrn_w_in": f(hgrn_w_in), "hgrn_lb_logits": f(hgrn_lb_logits),
        "hgrn_norm_w": f(hgrn_norm_w), "hgrn_w_out": f(hgrn_w_out),
    }
    x = f(x)
    nc = _get_prog(((0, 1, 2, 3), True, True))
    in_maps = [dict(shared, x=x[b]) for b in range(n)]
    res = run_bass_kernel_spmd(nc, in_maps, core_ids=list(range(n)))
    return np.stack([np.asarray(r["out"]) for r in res.results], axis=0).astype(np.float32)
```
